# Optimizing a Trainium2 kernel written in Bass

```python
import jax, jax.numpy as jnp
from jax import lax
import numpy as np

D_MODEL = 1024
BATCH = 4
SEQ = 8192
DEPTH = 4
DEC_BATCH = 2
DEC_SEQ = 8192
PAST_LEN = 128

QK_NOPE_DIM = 128
QK_ROPE_DIM = 64
V_HEAD_DIM = 128
N_HEADS = D_MODEL // V_HEAD_DIM
ATTN_WIDTH = N_HEADS * V_HEAD_DIM
Q_LORA_RANK = 256
KV_LORA_RANK = 128
ROPE_THETA = 10000.0
Q_BLOCK = 128
POOL_WINDOWS = (2, 4, 8, 16)
N_POOL_GROUPS = 4
POOL_WIDTH = D_MODEL
POOL_GROUP = POOL_WIDTH // N_POOL_GROUPS
IN_WIDTH = Q_LORA_RANK + KV_LORA_RANK + QK_ROPE_DIM + POOL_WIDTH + ATTN_WIDTH + POOL_WIDTH
D_FF = 2816
N_MOD = 9
EPS = 1e-6

kernel_name = 'hybrid_mla_pool_macaron_encoder'


def rmsnorm(x, g):
    x32 = x.astype(jnp.float32)
    y = x32 * lax.rsqrt(jnp.mean(x32 * x32, axis=-1, keepdims=True) + EPS) * g.astype(jnp.float32)
    return y.astype(x.dtype)


def modulate(h, shift, scale):
    return h * (1 + scale) + shift


def swiglu(h, w_gu, w_down):
    gu = h @ w_gu
    gate, up = gu[..., :D_FF], gu[..., D_FF:]
    return (jax.nn.silu(gate) * up) @ w_down


def rope_tables(seq_len, dtype):
    inv = 1.0 / (ROPE_THETA ** (jnp.arange(0, QK_ROPE_DIM, 2, dtype=jnp.float32) / QK_ROPE_DIM))
    ang = jnp.arange(seq_len, dtype=jnp.float32)[:, None] * inv[None, :]
    ang = jnp.concatenate([ang, ang], axis=-1)
    return jnp.cos(ang).astype(dtype), jnp.sin(ang).astype(dtype)


def apply_rope(x, cos, sin):
    x1, x2 = x[..., :QK_ROPE_DIM // 2], x[..., QK_ROPE_DIM // 2:]
    return x * cos + jnp.concatenate([-x2, x1], axis=-1) * sin


def latent_attention(q_nope, q_rope, k_nope, k_rope, v):
    B, S, H, _ = q_nope.shape
    nb = S // Q_BLOCK
    scale = (QK_NOPE_DIM + QK_ROPE_DIM) ** -0.5

    def to_blocks(t):
        return jnp.moveaxis(t.reshape(B, nb, Q_BLOCK, *t.shape[2:]), 1, 0)

    def attend(blk):
        qn, qr = blk
        s = (jnp.einsum('bqhd,bkhd->bhqk', qn, k_nope).astype(jnp.float32)
             + jnp.einsum('bqhd,bkd->bhqk', qr, k_rope).astype(jnp.float32))
        p = jax.nn.softmax(s * scale, axis=-1).astype(v.dtype)
        return jnp.einsum('bhqk,bkhd->bqhd', p, v)

    o = lax.map(attend, (to_blocks(q_nope), to_blocks(q_rope)))
    return jnp.moveaxis(o, 0, 1).reshape(B, S, H * V_HEAD_DIM)


def multiscale_pool(u, w_pool, pool_scale):
    B, S, _ = u.shape
    u32 = u.astype(jnp.float32).reshape(B, S, N_POOL_GROUPS, POOL_GROUP)
    cs = jnp.concatenate([jnp.zeros((B, 1, N_POOL_GROUPS, POOL_GROUP), jnp.float32),
                          jnp.cumsum(u32, axis=1)], axis=1)
    half = jnp.array(POOL_WINDOWS, dtype=jnp.int32) // 2
    t = jnp.arange(S, dtype=jnp.int32)[:, None]
    lo = jnp.clip(t - half[None, :], 0, S)
    hi = jnp.clip(t + half[None, :], 0, S)
    g = jnp.arange(N_POOL_GROUPS, dtype=jnp.int32)[None, :]
    win_sum = cs[:, hi, g, :] - cs[:, lo, g, :]
    mean = win_sum / (hi - lo).astype(jnp.float32)[None, :, :, None]
    pooled = (mean - u32).astype(u.dtype)
    mixed = jnp.einsum('bsgc,gcd->bsgd', pooled, w_pool).reshape(B, S, POOL_WIDTH)
    return mixed * pool_scale


def hybrid_mixer(h, cos, sin, w_in, q_a_norm, w_qb, kv_a_norm, w_kvb, w_pool, pool_scale, w_out):
    B, S, _ = h.shape
    z = h @ w_in
    o1 = Q_LORA_RANK
    o2 = o1 + KV_LORA_RANK
    o3 = o2 + QK_ROPE_DIM
    o4 = o3 + POOL_WIDTH
    q_a, kv_a, k_rope, u, gate_logits = z[..., :o1], z[..., o1:o2], z[..., o2:o3], z[..., o3:o4], z[..., o4:]
    q = (rmsnorm(q_a, q_a_norm) @ w_qb).reshape(B, S, N_HEADS, QK_NOPE_DIM + QK_ROPE_DIM)
    q_nope, q_rope = q[..., :QK_NOPE_DIM], q[..., QK_NOPE_DIM:]
    kv = (rmsnorm(kv_a, kv_a_norm) @ w_kvb).reshape(B, S, N_HEADS, QK_NOPE_DIM + V_HEAD_DIM)
    k_nope, v = kv[..., :QK_NOPE_DIM], kv[..., QK_NOPE_DIM:]
    q_rope = apply_rope(q_rope, cos[:, None, :], sin[:, None, :])
    k_rope = apply_rope(k_rope, cos, sin)
    o_attn = latent_attention(q_nope, q_rope, k_nope, k_rope, v)
    o_pool = multiscale_pool(u, w_pool, pool_scale)
    gates = jax.nn.sigmoid(gate_logits)
    g_attn, g_pool = gates[..., :ATTN_WIDTH], gates[..., ATTN_WIDTH:]
    return (g_attn * o_attn + g_pool * o_pool) @ w_out


def encoder_trunk(x, c, w_ada, b_ada, n1_pre, w1_gu, w1_down, n1_post,
                  nm_pre, w_in, q_a_norm, w_qb, kv_a_norm, w_kvb, w_pool, pool_scale, w_out, nm_post,
                  n2_pre, w2_gu, w2_down, n2_post):
    B, S, D = x.shape
    cos, sin = rope_tables(S, x.dtype)
    for l in range(DEPTH):
        mod = (jax.nn.silu(c) @ w_ada[l] + b_ada[l]).reshape(B, N_MOD, D)[:, :, None, :]
        h = modulate(rmsnorm(x, n1_pre[l]), mod[:, 0], mod[:, 1])
        x = x + 0.5 * mod[:, 2] * rmsnorm(swiglu(h, w1_gu[l], w1_down[l]), n1_post[l])
        h = modulate(rmsnorm(x, nm_pre[l]), mod[:, 3], mod[:, 4])
        m = hybrid_mixer(h, cos, sin, w_in[l], q_a_norm[l], w_qb[l], kv_a_norm[l], w_kvb[l],
                         w_pool[l], pool_scale[l], w_out[l])
        x = x + mod[:, 5] * rmsnorm(m, nm_post[l])
        h = modulate(rmsnorm(x, n2_pre[l]), mod[:, 6], mod[:, 7])
        x = x + 0.5 * mod[:, 8] * rmsnorm(swiglu(h, w2_gu[l], w2_down[l]), n2_post[l])
    return x


def setup_inputs(seed: int = 0) -> dict:
    key = jax.random.key(seed)
    ks = jax.random.split(key, 32)
    f32 = jnp.float32
    L, D = DEPTH, D_MODEL

    def nrm(k, shape, scale):
        return jax.random.normal(k, shape, f32) * scale

    def gain(k, shape):
        return 1.0 + 0.02 * jax.random.normal(k, shape, f32)

    return {
        'x_prompt': nrm(ks[0], (BATCH, SEQ, D), 1.0),
        'x_sample': nrm(ks[1], (DEC_BATCH, DEC_SEQ, D), 1.0),
        'c_prompt': nrm(ks[2], (BATCH, D), 1.0),
        'c_sample': nrm(ks[3], (DEC_BATCH, D), 1.0),
        'w_ada': nrm(ks[4], (L, D, N_MOD * D), 0.3 * D ** -0.5),
        'b_ada': nrm(ks[5], (L, N_MOD * D), 0.02),
        'n1_pre': gain(ks[6], (L, D)),
        'w1_gu': nrm(ks[7], (L, D, 2 * D_FF), D ** -0.5),
        'w1_down': nrm(ks[8], (L, D_FF, D), D_FF ** -0.5),
        'n1_post': gain(ks[9], (L, D)),
        'nm_pre': gain(ks[10], (L, D)),
        'w_in': nrm(ks[11], (L, D, IN_WIDTH), D ** -0.5),
        'q_a_norm': gain(ks[12], (L, Q_LORA_RANK)),
        'w_qb': nrm(ks[13], (L, Q_LORA_RANK, N_HEADS * (QK_NOPE_DIM + QK_ROPE_DIM)), Q_LORA_RANK ** -0.5),
        'kv_a_norm': gain(ks[14], (L, KV_LORA_RANK)),
        'w_kvb': nrm(ks[15], (L, KV_LORA_RANK, N_HEADS * (QK_NOPE_DIM + V_HEAD_DIM)), KV_LORA_RANK ** -0.5),
        'w_pool': nrm(ks[16], (L, N_POOL_GROUPS, POOL_GROUP, POOL_GROUP), POOL_GROUP ** -0.5),
        'pool_scale': 1.0 + 0.05 * jax.random.normal(ks[17], (L, POOL_WIDTH), f32),
        'w_out': nrm(ks[18], (L, D, D), D ** -0.5),
        'nm_post': gain(ks[19], (L, D)),
        'n2_pre': gain(ks[20], (L, D)),
        'w2_gu': nrm(ks[21], (L, D, 2 * D_FF), D ** -0.5),
        'w2_down': nrm(ks[22], (L, D_FF, D), D_FF ** -0.5),
        'n2_post': gain(ks[23], (L, D)),
    }


def reference(x_prompt, x_sample, c_prompt, c_sample, w_ada, b_ada, n1_pre, w1_gu, w1_down, n1_post,
              nm_pre, w_in, q_a_norm, w_qb, kv_a_norm, w_kvb, w_pool, pool_scale, w_out, nm_post,
              n2_pre, w2_gu, w2_down, n2_post):
    y_prompt = encoder_trunk(x_prompt, c_prompt, w_ada, b_ada, n1_pre, w1_gu, w1_down, n1_post,
                             nm_pre, w_in, q_a_norm, w_qb, kv_a_norm, w_kvb, w_pool, pool_scale, w_out, nm_post,
                             n2_pre, w2_gu, w2_down, n2_post)
    y_sample = encoder_trunk(x_sample, c_sample, w_ada, b_ada, n1_pre, w1_gu, w1_down, n1_post,
                             nm_pre, w_in, q_a_norm, w_qb, kv_a_norm, w_kvb, w_pool, pool_scale, w_out, nm_post,
                             n2_pre, w2_gu, w2_down, n2_post)
    return (y_prompt, y_sample)
```

```python
import numpy as np
from contextlib import ExitStack
import concourse.bass as bass
import concourse.mybir as mybir
from concourse.bass_utils import run_bass_kernel_spmd

F32 = mybir.dt.float32
BF16 = mybir.dt.bfloat16
AF = mybir.ActivationFunctionType
ALU = mybir.AluOpType
AX = mybir.AxisListType

D = 1024
DFF = 2816
NH = 8
T = 512
NV = 131
EPS = 1e-6
SCALE = 192.0 ** -0.5
RING = 4
LIVE = 2
PAD = 16
USE_B = True
USE_B_FFN = True


class Tracker:
    def __init__(self, nc, es):
        self.nc = nc
        self.es = es
        self.eng = {"pe": nc.tensor, "act": nc.scalar, "dve": nc.vector, "pool": nc.gpsimd, "sp": nc.sync}
        self.sem = {}
        self.cnt = {}
        for e in ("pe", "act", "dve", "pool"):
            self.sem[e] = es.enter_context(nc.semaphore("s_" + e))
            self.cnt[e] = 0
        self.waited = {}
        self.last_w = {}
        self.readers = {}
        self.n_ins = 0

    def dsem(self, name):
        k = "d:" + name
        if k not in self.sem:
            self.sem[k] = self.es.enter_context(self.nc.semaphore("d_" + name))
            self.cnt[k] = 0
        return k

    def _deps(self, r, w):
        deps = {}

        def add(tok):
            if tok is None:
                return
            k, v = tok
            if deps.get(k, 0) < v:
                deps[k] = v

        for key in r:
            add(self.last_w.get(key))
        for key in w:
            add(self.last_w.get(key))
            for tok in self.readers.get(key, {}).items():
                add(tok)
        return deps

    def _wait(self, e, deps, keep_last=False):
        eng = self.eng[e]
        todo = []
        for k, v in deps.items():
            if k == e and (e == "pe" or v <= self.cnt[e] - 2):
                continue
            if self.waited.get((e, k), 0) >= v:
                continue
            if k.startswith("d:"):
                assert self.cnt[k] == v, ("dma sem ambiguity", k, v, self.cnt[k])
            todo.append((k, v))
            self.waited[(e, k)] = v
        last = None
        if keep_last and todo:
            last = todo.pop()
        for k, v in todo:
            eng.wait_ge(self.sem[k], v)
            self.n_ins += 1
        return last

    def _record(self, tok, r, w):
        for key in r:
            self.readers.setdefault(key, {})
            d = self.readers[key]
            if d.get(tok[0], 0) < tok[1]:
                d[tok[0]] = tok[1]
        for key in w:
            self.last_w[key] = tok
            self.readers[key] = {}

    def op(self, e, fn, r=(), w=(), sig=True, wait_w=()):
        last = self._wait(e, self._deps(r, list(w) + list(wait_w)), keep_last=True)
        ins = fn(self.eng[e])
        if last is not None:
            ins._wait_ge(self.sem[last[0]], last[1])
        self.n_ins += 1
        if sig:
            self.cnt[e] += 1
            ins.then_inc(self.sem[e], 1)
            tok = (e, self.cnt[e])
        else:
            tok = (e, self.cnt[e] + 1)
        self._record(tok, r, w)
        return tok

    def dma(self, q, out, in_, r=(), w=(), sem="x", **kw):
        k = self.dsem(sem)
        self._wait(q, self._deps(r, w))
        ins = self.eng[q].dma_start(out=out, in_=in_, **kw)
        self.n_ins += 1
        self.cnt[k] += 16
        ins.then_inc(self.sem[k], 16)
        tok = (k, self.cnt[k])
        self._record(tok, r, w)
        return tok

    def wait_all(self, e, keys):
        self._wait(e, self._deps(keys, keys))


class Builder:
    def __init__(self, S, L, debug=False):
        assert S % T == 0
        self.S, self.L = S, L
        self.debug = debug
        self._dumped = set()
        self.NT = S // T
        self.NKT = S // 128
        self.nc = bass.Bass("TRN2", target_bir_lowering=False)
        self.es = ExitStack()
        self.tr = Tracker(self.nc, self.es)
        self._declare()

    def _dram(self, name, shape, dt, kind):
        return self.nc.dram_tensor(name, list(shape), dt, kind=kind).ap()

    def _sb(self, name, shape, dt):
        return self.es.enter_context(self.nc.sbuf_tensor(name, list(shape), dt))

    def _declare(self):
        S, L = self.S, self.L
        I = "ExternalInput"
        self.xT = self._dram("xT", [D, S], F32, I)
        self.cT = self._dram("cT", [128, 8], F32, I)
        self.vecs = self._dram("vecs", [128, L * NV], F32, I)
        self.w_ada = self._dram("w_ada", [L, D, 9 * D], F32, I)
        self.w1_gu = self._dram("w1_gu", [L, D, 2 * DFF], F32, I)
        self.w1_down = self._dram("w1_down", [L, DFF, D], F32, I)
        self.w2_gu = self._dram("w2_gu", [L, D, 2 * DFF], F32, I)
        self.w2_down = self._dram("w2_down", [L, DFF, D], F32, I)
        self.w_in = self._dram("w_in", [L, D, 3520], F32, I)
        self.w_qb = self._dram("w_qb", [L, 256, 1536], F32, I)
        self.w_kvb = self._dram("w_kvb", [L, 128, 2048], F32, I)
        self.wkT = self._dram("wkT", [L, 128, 1024], F32, I)
        self.w_pool = self._dram("w_pool", [L, 4, 256, 256], F32, I)
        self.w_out = self._dram("w_out", [L, D, D], F32, I)
        self.cosT = self._dram("cosT", [64, S], F32, I)
        self.sinT = self._dram("sinT", [64, S], F32, I)
        self.invc = self._dram("invc", [128, 4, S], F32, I)
        self.ident = self._dram("ident", [128, 128], F32, I)
        self.yT = self._dram("yT", [D, S], F32, "ExternalOutput")
        N = "ExternalOutput" if self.debug else "Internal"
        E = 4096
        self.s_gu1 = self._dram("s_gu1", [L, 11, 128, E], BF16, N)
        self.s_gu2 = self._dram("s_gu2", [L, 11, 128, E], BF16, N)
        self.s_d1 = self._dram("s_d1", [L, 8, 128, 22 * 128], BF16, N)
        self.s_d2 = self._dram("s_d2", [L, 8, 128, 22 * 128], BF16, N)
        self.s_in = self._dram("s_in", [L, 8, 128, E], BF16, N)
        self.s_qb = self._dram("s_qb", [L, 128, E], BF16, N)
        self.s_kv = self._dram("s_kv", [L, 128, 2048], BF16, N)
        self.s_pool = self._dram("s_pool", [L, 128, 2048], BF16, N)
        self.s_out = self._dram("s_out", [L, 2, 128, E], BF16, N)
        self.x1_s = self._dram("x1_s", [D, S], F32, N)
        self.u_s = self._dram("u_s", [L, D, S + 2 * PAD], F32, N)
        self.cT_s = self._dram("cT_s", [L, 128, S], BF16, N)
        self.kr_s = self._dram("kr_s", [L, 64, S], BF16, N)
        self.ctok_s = self._dram("ctok_s", [L, S, 128], BF16, N)

        sb = self._sb
        self.ones_bf = sb("ones_bf", [128, 128], BF16)
        self.one1_bf = sb("one1_bf", [128, 128], BF16)
        self.ones_f = sb("ones_f", [128, 128], F32)
        self.ident_bf = sb("ident_bf", [128, 128], BF16)
        self.vecs_sb = sb("vecs_sb", [128, L * NV], F32)
        self.mod = sb("mod", [128, L * 72], F32)
        self.coef = sb("coef", [128, L * 72], F32)
        self.cT_sb = sb("cT_sb", [128, 8], F32)
        self.sc_sb = sb("sc_sb", [128, 8], F32)
        self.kmax2 = sb("kmax2", [128, L], F32)
        self.small = sb("small", [128, 16], F32)
        self.negM = sb("negM", [128, 8], F32)
        self.epsb = sb("epsb", [128, 1], F32)
        self.oneb = sb("oneb", [128, 1], F32)
        self.Kc = sb("Kc", [128, S], BF16)
        self.Kt = sb("Kt", [128, S], BF16)
        self.Kr = sb("Kr", [128, S], BF16)
        self.xt = sb("xt", [128, 8, T], F32)
        self.hb = sb("hb", [128, 8, T], BF16)
        self.sq = sb("sq", [128, 2, T], BF16)
        self.rstd = sb("rstd", [128, T], F32)
        self.tmpf = sb("tmpf", [128, 2, T], F32)
        self.yt = sb("yt", [128, 8, T], F32)
        self.ring = [sb("ring%d" % i, [128, 4096], BF16) for i in range(RING)]
        self.regA = sb("regA", [128, 12288], BF16)
        self.qaT = sb("qaT", [128, 2, T], F32)
        self.ut = sb("ut", [128, 2, 2, T + 2 * PAD], F32)
        self.pw = sb("pw", [128, 2, T + 2 * PAD], F32)
        self.ovs = sb("ovs", [128, T], F32)
        self.invc_sb = sb("invc_sb", [128, 2, T], F32)
        self.PTs = sb("PTs", [128, 6 * T], BF16)
        self.accd = sb("accd", [128, 2 * T], F32)
        self.olat = sb("olat", [128, T], BF16)
        self.ptsum = sb("ptsum", [128, T], BF16)
        self.merged = sb("merged", [128, 8, T], BF16)
        self.wv_sb = sb("wv_sb", [128, 1024], BF16)
        self.cos_sb = sb("cos_sb", [64, T], F32)
        self.sin_sb = sb("sin_sb", [64, T], F32)
        self.ps = self.es.enter_context(self.nc.psum_tensor("ps", [128, 8 * T], F32))
        A = self.regA
        self.aT = A[:, 0:22 * T].rearrange("p (f t) -> p f t", t=T)
        self.qlatT = A[:, 0:8 * T].rearrange("p (h t) -> p h t", t=T)
        self.qrT = A[0:64, 8 * T:16 * T].rearrange("p (h t) -> p h t", t=T)
        self.qrT_full = A[:, 8 * T:16 * T].rearrange("p (h t) -> p h t", t=T)
        self.qrT_hi = A[64:128, 8 * T:16 * T]
        self.qnb = A[:, 16 * T:18 * T].rearrange("p (k t) -> p k t", t=T)
        self.qnope = A[:, 18 * T:20 * T].rearrange("p (k t) -> p k t", t=T)
        self.krt = A[0:64, 20 * T:21 * T]
        self.ctile = A[:, 21 * T:22 * T]
        self.ctk = A[:, 22 * T:23 * T]
        self.rtmp = self.yt[:, 0:4, :].rearrange("p (s two) t -> p s two t", two=2)
        self.pooled = self.ovs[:].bitcast(BF16).rearrange("p (a t) -> p a t", a=2)
        self.ytb = self.yt[:].rearrange("p k t -> p (k t)").bitcast(BF16)[:, 0:8 * T].rearrange("p (k t) -> p k t", t=T)
        self.wpos = 0
        self.wplan = []

    def dump(self, name, ap, keys, dt=F32):
        if not self.debug or name in self._dumped:
            return
        self._dumped.add(name)
        d = self._dram("dbg_" + name, list(ap.shape), dt, "ExternalOutput")
        self.tr.dma("pool", d, ap, r=keys, w=["dbg_" + name], sem="dbg_" + name)

    def bank(self, b, n=1):
        return self.ps[:, b * T:(b + n) * T]

    def coefv(self, l, which):
        return self.coef[:, l * 72 + which * 8: l * 72 + which * 8 + 8]

    def vec(self, l, off, n=8):
        return self.vecs_sb[:, l * NV + off: l * NV + off + n]

    def plan(self, items):
        self.wplan.extend(items)

    def _issue(self, pos):
        if pos >= len(self.wplan):
            return
        name, ap, ne, lay = self.wplan[pos]
        s = pos % RING
        self.tr.dma("sp", self.ring[s][:, 0:ne], ap, r=["wcast%d" % lay], w=["ring%d" % s], sem="ring%d" % s)

    def wnext(self, name):
        pos = self.wpos
        assert self.wplan[pos][0] == name, (self.wplan[pos][0], name)
        if pos == 0:
            for p in range(RING - LIVE):
                self._issue(p)
        self._issue(pos + RING - LIVE)
        self.wpos += 1
        s = pos % RING
        return self.ring[s], "ring%d" % s

    def prologue(self):
        tr, L = self.tr, self.L
        g = "pool"
        tr.op("dve", lambda e: e.memset(self.ones_bf[:], 1.0 / 1024.0), w=["ones_bf"])
        tr.op("dve", lambda e: e.memset(self.one1_bf[:], 1.0), w=["one1_bf"])
        tr.op("dve", lambda e: e.memset(self.ones_f[:], 1.0), w=["ones_f"])
        tr.op("dve", lambda e: e.memset(self.kmax2[:], 0.0), w=["kmax2"])
        tr.op("dve", lambda e: e.memset(self.epsb[:], EPS), w=["epsb"])
        tr.op("dve", lambda e: e.memset(self.Kr[64:128, :], 0.0), w=["Krz"])
        tr.op("dve", lambda e: e.memset(self.oneb[:], 1.0), w=["oneb"])
        tr.dma(g, self.ident_bf[:], self.ident[:, :], w=["ident_bf"], sem="c0")
        tr.dma("sp", self.vecs_sb[:], self.vecs[:, :], w=["vecs"], sem="c1")
        tr.dma("sp", self.cT_sb[:], self.cT[:, :], w=["cT"], sem="c2")
        tr.op("dve", lambda e: e.memset(self.pw[:, 0, 0:PAD], 0.0), w=["pw"])
        for l in range(L):
            for side in (0, 1):
                c0 = 0 if side == 0 else self.S + PAD
                tr.dma(g, self.u_s[l].rearrange("(kc p) s -> p kc s", p=128)[:, :, c0:c0 + PAD],
                       self.pw[:, 0:1, 0:PAD].to_broadcast([128, 8, PAD]), r=["pw"], w=["upad"], sem="upad")
        self._cast_layer(0)
        tr.op("act", lambda e: e.activation(out=self.sc_sb[:], in_=self.cT_sb[:], func=AF.Silu),
              r=["cT"], w=["sc"])
        stage = [self.xt, self.yt]
        rows = [(self.rstd[0:1, :], "rstd"), (self.tmpf[0:1, 0, :], "tmpf0")]
        blk = 0
        for l in range(L):
            for cb in range(18):
                st = stage[blk % 2]
                key = "xt" if blk % 2 == 0 else "yt"
                stv = st[:].rearrange("p k t -> p (k t)")[:, 0:4096].rearrange("p (k c) -> p k c", k=8)
                tr.dma("sp", stv, self.w_ada[l].rearrange("(kc p) f -> p kc f", p=128)[:, :, cb * 512:(cb + 1) * 512],
                       w=[key], sem="stage%d" % (blk % 2))
                br = blk % 6
                for kc in range(8):
                    tr.op("pe", lambda e, kc=kc, br=br, stv=stv: e.matmul(
                        self.bank(br)[0:1, :], lhsT=self.sc_sb[:, kc:kc + 1], rhs=stv[:, kc, :],
                        start=(kc == 0), stop=(kc == 7)), r=[key, "sc"], w=["ps%d" % br], sig=(kc == 7))
                rowap, rowk = rows[blk % 2]
                tr.op("act", lambda e, br=br, rowap=rowap: e.activation(out=rowap, in_=self.bank(br)[0:1, :], func=AF.Copy),
                      r=["ps%d" % br], w=[rowk])
                for jj in range(4):
                    j = cb * 4 + jj
                    tr.op("pe", lambda e, jj=jj, j=j, rowap=rowap: e.matmul(
                        self.bank(7)[:, j:j + 1], lhsT=rowap[0:1, jj * 128:(jj + 1) * 128], rhs=self.ones_f[0:1, 0:1],
                        start=True, stop=True), r=[rowk, "ones_f"], w=["ps7"], sig=(jj == 3))
                blk += 1
            tr.op("dve", lambda e, l=l: e.tensor_tensor(out=self.mod[:, l * 72:(l + 1) * 72], in0=self.bank(7)[:, 0:72],
                                                        in1=self.vec(l, 59, 72), op=ALU.add),
                  r=["ps7", "vecs"], w=["mod"])
            for si, (pre, post, half) in enumerate(((0, 8, 0.5), (16, 24, 1.0), (32, 40, 0.5))):
                m = lambda i, l=l: self.mod[:, l * 72 + i * 8: l * 72 + i * 8 + 8]
                tr.op("dve", lambda e, si=si, pre=pre, l=l, m=m: e.scalar_tensor_tensor(
                    out=self.coefv(l, si * 3 + 0), in0=m(si * 3 + 1), scalar=1.0, in1=self.vec(l, pre),
                    op0=ALU.add, op1=ALU.mult), r=["mod", "vecs"], w=["coef"])
                tr.op("dve", lambda e, si=si, l=l, m=m: e.tensor_copy(out=self.coefv(l, si * 3 + 1), in_=m(si * 3 + 0)),
                      r=["mod"], w=["coef"])
                tr.op("dve", lambda e, si=si, post=post, half=half, l=l, m=m: e.scalar_tensor_tensor(
                    out=self.coefv(l, si * 3 + 2), in0=m(si * 3 + 2), scalar=half, in1=self.vec(l, post),
                    op0=ALU.mult, op1=ALU.mult), r=["mod", "vecs"], w=["coef"])

    def _cast(self, out, in_):
        self.tr.dma("pool", out, in_, w=["wcast%d" % self._cl], sem="wcast%d" % self._cl)

    def _cast_layer(self, l):
        self._cl = l
        def gu(dst, src):
            for j in range(11):
                for two in range(2):
                    self._cast(dst[l, j].rearrange("p (kc c) -> p kc c", kc=8)[:, :, two * 256:(two + 1) * 256],
                               src[l].rearrange("(kc p) f -> p kc f", p=128)[:, :, two * DFF + j * 256: two * DFF + (j + 1) * 256])

        def down(dst, src):
            for d in range(8):
                self._cast(dst[l, d].rearrange("p (fc m) -> p fc m", m=128),
                           src[l].rearrange("(fc p) d -> p fc d", p=128)[:, :, d * 128:(d + 1) * 128])

        win = self.w_in[l].rearrange("(kc p) f -> p kc f", p=128)

        def inslot(si, c0, n, o=0, width=512):
            self._cast(self.s_in[l, si][:, 0:8 * width].rearrange("p (kc c) -> p kc c", kc=8)[:, :, o:o + n], win[:, :, c0:c0 + n])

        gu(self.s_gu1, self.w1_gu)
        down(self.s_d1, self.w1_down)
        inslot(0, 0, 256, width=256)
        inslot(1, 256, 192, width=256)
        inslot(1, 416, 32, 192, width=256)
        inslot(1, 384, 32, 224, width=256)
        inslot(2, 448, 512); inslot(3, 960, 512)
        inslot(4, 1472, 512); inslot(5, 1984, 512)
        inslot(6, 2496, 512); inslot(7, 3008, 512)
        qb = self.s_qb[l].rearrange("p (kc c) -> p kc c", kc=2)
        wq = self.w_qb[l].rearrange("(kc p) f -> p kc f", p=128)
        self._cast(qb[:, :, 0:1536], wq)
        qbp = qb[:, :, 1536:2048].rearrange("p k (h e) -> p k h e", e=64)
        wqh = wq.rearrange("p k (h e) -> p k h e", e=192)
        for kc in range(2):
            self._cast(qbp[:, kc, :, 0:32], wqh[:, kc, :, 160:192])
            self._cast(qbp[:, kc, :, 32:64], wqh[:, kc, :, 128:160])
        self._cast(self.s_kv[l][:, 0:1024], self.wkT[l])
        self._cast(self.s_kv[l][:, 1024:2048].rearrange("p (h e) -> p h e", e=128),
                   self.w_kvb[l].rearrange("p (h e) -> p h e", e=256)[:, :, 128:256])
        self._cast(self.s_pool[l].rearrange("p (g kc m) -> p g kc m", g=4, kc=2),
                   self.w_pool[l].rearrange("g (kc p) m -> p g kc m", p=128))
        for s in range(2):
            self._cast(self.s_out[l, s].rearrange("p (kc c) -> p kc c", kc=8),
                       self.w_out[l].rearrange("(kc p) f -> p kc f", p=128)[:, :, s * 512:(s + 1) * 512])
        gu(self.s_gu2, self.w2_gu)
        down(self.s_d2, self.w2_down)

    def plan_ffn(self, l, which):
        gu = self.s_gu1 if which == 1 else self.s_gu2
        dn = self.s_d1 if which == 1 else self.s_d2
        return ([("gu%d_%d" % (which, j), gu[l, j], 4096, l) for j in range(11)]
                + [("dn%d_%d" % (which, d), dn[l, d], 22 * 128, l) for d in range(8)])

    def plan_kside(self, l):
        return [("K1", self.s_in[l, 1][:, 0:2048], 2048, l), ("U0", self.s_in[l, 2][:, :], 4096, l), ("U1", self.s_in[l, 3][:, :], 4096, l)]

    def plan_mixer(self, l):
        return [("QA", self.s_in[l, 0][:, 0:2048], 2048, l), ("WQB", self.s_qb[l][:, :], 4096, l), ("WKV", self.s_kv[l][:, :], 2048, l),
                ("WPOOL0", self.s_pool[l][:, :], 2048, l), ("GP0", self.s_in[l, 6][:, :], 4096, l),
                ("WPOOL1", self.s_pool[l][:, :], 2048, l), ("GP1", self.s_in[l, 7][:, :], 4096, l),
                ("GA0", self.s_in[l, 4][:, :], 4096, l), ("GA1", self.s_in[l, 5][:, :], 4096, l),
                ("WO0", self.s_out[l, 0][:, :], 4096, l), ("WO1", self.s_out[l, 1][:, :], 4096, l)]

    def sumsq(self, srcs, bank_ap, ones, keys_r, wide=None, sq_eng="alt"):
        tr = self.tr
        n = len(srcs)
        for i, (ap, rows, key) in enumerate(srcs):
            s = i % 2
            if wide is not None:
                dst, dkey, ww = wide[0][0:rows, i, :], "%s:%d" % (wide[1], i), [wide[1]]
            else:
                dst, dkey, ww = self.sq[0:rows, s, :], "sq%d" % s, []
            if s == 0 or sq_eng == "act":
                tr.op("act", lambda e, ap=ap, dst=dst: e.activation(out=dst, in_=ap, func=AF.Square), r=[key], w=[dkey], wait_w=ww)
            else:
                tr.op("dve", lambda e, ap=ap, dst=dst: e.tensor_tensor(out=dst, in0=ap, in1=ap, op=ALU.mult), r=[key], w=[dkey], wait_w=ww)
        for i, (ap, rows, key) in enumerate(srcs):
            s = i % 2
            if wide is not None:
                src, skeys = wide[0][0:rows, i, :], ["%s:%d" % (wide[1], i), wide[1]]
            else:
                src, skeys = self.sq[0:rows, s, :], ["sq%d" % s]
            tr.op("pe", lambda e, src=src, rows=rows, i=i: e.matmul(bank_ap, lhsT=ones[0:rows, :], rhs=src,
                                                                    start=(i == 0), stop=(i == n - 1)),
                  r=skeys + ["ones_bf", "one1_bf"], w=keys_r, sig=(i == n - 1))

    def rsqrt_bank(self, bank_ap, key, mult):
        tr = self.tr
        tr.op("act", lambda e: e.activation(out=self.rstd[:], in_=bank_ap, func=AF.Ln, bias=self.epsb[:, 0:1], scale=mult),
              r=[key, "epsb"], w=["rstd"])
        tr.op("act", lambda e: e.activation(out=self.rstd[:], in_=self.rstd[:], func=AF.Exp, scale=-0.5),
              r=["rstd"], w=["rstd"])

    def prenorm(self, l, which, xkey="xt", chained=True):
        tr = self.tr
        self.sumsq([(self.xt[:, kc, :], 128, ("xt:%d" % kc) if chained else xkey) for kc in range(8)], self.bank(7), self.ones_bf, ["ps7"],
                   wide=(self.qlatT, "sqw"), sq_eng=("act" if chained else "alt"))
        self.rsqrt_bank(self.bank(7), "ps7", 1.0)
        Acol = self.coefv(l, which * 3 + 0)
        Bcol = self.coefv(l, which * 3 + 1)
        for kc in range(8):
            s = kc % 2
            tr.op("dve", lambda e, kc=kc, s=s: e.tensor_tensor(out=self.tmpf[:, s, :], in0=self.xt[:, kc, :], in1=self.rstd[:], op=ALU.mult),
                  r=[xkey, "rstd"], w=["tmpf%d" % s])
            tr.op("act", lambda e, kc=kc, s=s: e.activation(out=self.hb[:, kc, :], in_=self.tmpf[:, s, :], func=AF.Identity,
                                                            bias=Bcol[:, kc:kc + 1], scale=Acol[:, kc:kc + 1]),
                  r=["tmpf%d" % s, "coef"], w=["hb"])

    def post_sq(self, d, bk):
        self.tr.op("act", lambda e: e.activation(out=self.hb[:, d, :], in_=self.bank(bk), func=AF.Square),
                   r=["ps%d" % bk], w=["hb:%d" % d], wait_w=["hb"])

    def post_mm(self, d):
        self.tr.op("pe", lambda e: e.matmul(self.bank(7), lhsT=self.ones_bf[:, :], rhs=self.hb[:, d, :],
                                            start=(d == 0), stop=(d == 7)),
                   r=["hb:%d" % d, "hb", "ones_bf"], w=["ps7"], sig=(d == 7))

    def postnorm(self, l, which, stats_done=False):
        tr = self.tr
        if not stats_done:
            self.sumsq([(self.yt[:, kc, :], 128, "yt") for kc in range(8)], self.bank(7), self.ones_bf, ["ps7"],
                       wide=(self.hb, "hb"))
        self.rsqrt_bank(self.bank(7), "ps7", 1.0)
        Gcol = self.coefv(l, which * 3 + 2)
        for kc in range(8):
            s = kc % 2
            tr.op("dve", lambda e, kc=kc, s=s: e.scalar_tensor_tensor(
                out=self.tmpf[:, s, :], in0=self.yt[:, kc, :], scalar=Gcol[:, kc:kc + 1], in1=self.rstd[:],
                op0=ALU.mult, op1=ALU.mult), r=["yt", "rstd", "coef"], w=["tmpf%d" % s])
            tr.op("dve", lambda e, kc=kc, s=s: e.tensor_tensor(
                out=self.xt[:, kc, :], in0=self.xt[:, kc, :], in1=self.tmpf[:, s, :], op=ALU.add),
                r=["tmpf%d" % s, "xt"], w=["xt", "xt:%d" % kc])

    def ffn(self, l, which, chained=True):
        tr = self.tr
        wi = 0 if which == 1 else 2
        self.prenorm(l, wi, chained=chained)
        pb = 0
        for j in range(11):
            slot, skey = self.wnext("gu%d_%d" % (which, j))
            sv = slot[:, 0:4096].rearrange("p (kc c) -> p kc c", kc=8)
            for fc in range(2):
                bg, bu = pb % 6, (pb + 1) % 6
                pb += 2
                for (bk, off) in ((bg, fc * 128), (bu, 256 + fc * 128)):
                    for kc in range(8):
                        tr.op("pe", lambda e, bk=bk, off=off, kc=kc, sv=sv: e.matmul(
                            self.bank(bk), lhsT=sv[:, kc, off:off + 128], rhs=self.hb[:, kc, :],
                            start=(kc == 0), stop=(kc == 7)), r=[skey, "hb"], w=["ps%d" % bk], sig=(kc == 7))
                s = fc
                tr.op("act", lambda e, bg=bg, s=s: e.activation(out=self.tmpf[:, s, :], in_=self.bank(bg), func=AF.Silu),
                      r=["ps%d" % bg], w=["tmpf%d" % s])
                f = 2 * j + fc
                tr.op("dve", lambda e, bu=bu, s=s, f=f: e.tensor_tensor(
                    out=self.aT[:, f, :], in0=self.bank(bu), in1=self.tmpf[:, s, :], op=ALU.mult),
                    r=["ps%d" % bu, "tmpf%d" % s], w=(["aT", "krt", "ctile"] if f >= 20 else ["aT"]))
        for d in range(8):
            slot, skey = self.wnext("dn%d_%d" % (which, d))
            sv = slot[:, 0:22 * 128].rearrange("p (fc m) -> p fc m", m=128)
            bk = d % 6
            for fc in range(22):
                tr.op("pe", lambda e, bk=bk, fc=fc, sv=sv: e.matmul(
                    self.bank(bk), lhsT=sv[:, fc, :], rhs=self.aT[:, fc, :], start=(fc == 0), stop=(fc == 21)),
                    r=[skey, "aT"], w=["ps%d" % bk], sig=(fc == 21))
            tr.op("act", lambda e, bk=bk, d=d: e.activation(out=self.yt[:, d, :], in_=self.bank(bk), func=AF.Copy),
                  r=["ps%d" % bk], w=["yt"])
            if USE_B_FFN:
                self.post_sq(d, bk)
                if d >= 1:
                    self.post_mm(d - 1)
        if USE_B_FFN:
            self.post_mm(7)
        self.postnorm(l, wi, stats_done=USE_B_FFN)

    def kside(self, l, i):
        tr = self.tr
        c0 = i * T
        self.prenorm(l, 1)
        tr.dma("pool", self.cos_sb[:], self.cosT[:, c0:c0 + T], w=["cos"], sem="cos")
        tr.dma("pool", self.sin_sb[:], self.sinT[:, c0:c0 + T], w=["sin"], sem="sin")
        slot, skey = self.wnext("K1")
        sv = slot[:, 0:2048].rearrange("p (kc c) -> p kc c", kc=8)
        for (bk, off, m) in ((0, 0, 128), (1, 128, 64), (2, 192, 64)):
            for kc in range(8):
                tr.op("pe", lambda e, bk=bk, off=off, m=m, kc=kc: e.matmul(
                    self.bank(bk)[0:m, :], lhsT=sv[:, kc, off:off + m], rhs=self.hb[:, kc, :],
                    start=(kc == 0), stop=(kc == 7)), r=[skey, "hb"], w=["ps%d" % bk], sig=(kc == 7))
        tr.op("act", lambda e: e.activation(out=self.qaT[:, 0, :], in_=self.bank(0), func=AF.Copy), r=["ps0"], w=["qaT"])
        self.sumsq([(self.qaT[:, 0, :], 128, "qaT")], self.bank(3), self.ones_bf, ["ps3"])
        self.rsqrt_bank(self.bank(3), "ps3", 8.0)
        tr.op("dve", lambda e: e.scalar_tensor_tensor(out=self.ctile, in0=self.qaT[:, 0, :],
                                                      scalar=self.vec(l, 58, 1), in1=self.rstd[:],
                                                      op0=ALU.mult, op1=ALU.mult), r=["qaT", "rstd", "vecs"], w=["ctile"])
        tr.op("dve", lambda e: e.tensor_tensor(out=self.tmpf[0:64, 0, :], in0=self.bank(1)[0:64, :], in1=self.cos_sb[:], op=ALU.mult),
              r=["ps1", "cos"], w=["tmpf0"])
        tr.op("dve", lambda e: e.tensor_tensor(out=self.tmpf[0:64, 1, :], in0=self.bank(2)[0:64, :], in1=self.sin_sb[:], op=ALU.mult),
              r=["ps2", "sin"], w=["tmpf1"])
        tr.op("dve", lambda e: e.tensor_tensor(out=self.krt, in0=self.tmpf[0:64, 0, :], in1=self.tmpf[0:64, 1, :], op=ALU.add),
              r=["tmpf0", "tmpf1"], w=["krt"])
        self.sumsq([(self.ctile, 128, "ctile"), (self.krt, 64, "krt")], self.bank(3), self.one1_bf, ["ps3"])
        tr.op("dve", lambda e: e.tensor_reduce(out=self.small[:, 0:1], in_=self.bank(3), axis=AX.X, op=ALU.max),
              r=["ps3"], w=["small0"])
        tr.op("dve", lambda e: e.tensor_tensor(out=self.kmax2[:, l:l + 1], in0=self.kmax2[:, l:l + 1], in1=self.small[:, 0:1], op=ALU.max),
              r=["small0", "kmax2"], w=["kmax2"])
        pb = self.bank(4).bitcast(BF16)
        for j in range(4):
            tr.op("pe", lambda e, j=j: e.transpose(pb[:, j * 128:(j + 1) * 128], self.ctile[:, j * 128:(j + 1) * 128], self.ident_bf[:]),
                  r=["ctile", "ident_bf"], w=["ps4"], sig=(j == 3))
        tr.op("act", lambda e: e.activation(out=self.ctk, in_=pb[:, 0:T], func=AF.Copy), r=["ps4"], w=["ctk"])
        tr.dma("pool", self.cT_s[l][:, c0:c0 + T], self.ctile, r=["ctile"], w=["cT_s%d" % l], sem="st_c")
        tr.dma("pool", self.kr_s[l][:, c0:c0 + T], self.krt, r=["krt"], w=["kr_s%d" % l], sem="st_kr")
        tr.dma("pool", self.ctok_s[l][c0:c0 + T, :].rearrange("(j p) c -> p j c", p=128),
               self.ctk.rearrange("p (j c) -> p j c", c=128), r=["ctk"], w=["ctok_s%d" % l], sem="st_ct")
        for half in range(2):
            slot, skey = self.wnext("U%d" % half)
            sv = slot[:, 0:4096].rearrange("p (kc c) -> p kc c", kc=8)
            for cc in range(4):
                ch = half * 4 + cc
                bk = ch % 6
                for kc in range(8):
                    tr.op("pe", lambda e, bk=bk, cc=cc, kc=kc, sv=sv: e.matmul(
                        self.bank(bk), lhsT=sv[:, kc, cc * 128:(cc + 1) * 128], rhs=self.hb[:, kc, :],
                        start=(kc == 0), stop=(kc == 7)), r=[skey, "hb"], w=["ps%d" % bk], sig=(kc == 7))
                eng = "act" if ch % 2 == 0 else "dve"
                if eng == "act":
                    tr.op("act", lambda e, bk=bk, ch=ch: e.activation(out=self.yt[:, ch, :], in_=self.bank(bk), func=AF.Copy),
                          r=["ps%d" % bk], w=["yt"])
                else:
                    tr.op("dve", lambda e, bk=bk, ch=ch: e.tensor_copy(out=self.yt[:, ch, :], in_=self.bank(bk)),
                          r=["ps%d" % bk], w=["yt"])
        tr.dma("pool", self.u_s[l].rearrange("(kc p) s -> p kc s", p=128)[:, :, PAD + c0:PAD + c0 + T], self.yt[:],
               r=["yt"], w=["u_s%d" % l], sem="st_u")

    def load_kside(self, l):
        tr = self.tr
        S = self.S
        tr.dma("pool", self.wv_sb[:], self.s_kv[l][:, 1024:2048], r=["wcast%d" % l], w=["wv_sb"], sem="ld_wv")
        tr.dma("pool", self.Kc[:], self.cT_s[l][:, :], r=["cT_s%d" % l], w=["Kc"], sem="ld_kc")
        tr.dma("pool", self.Kr[0:64, :], self.kr_s[l][:, :], r=["kr_s%d" % l], w=["Kr"], sem="ld_kr")
        tr.dma("pool", self.Kt[:].rearrange("p (n c) -> p n c", c=128),
               self.ctok_s[l].rearrange("(n p) c -> p n c", p=128), r=["ctok_s%d" % l], w=["Kt"], sem="ld_kt")

    def mixer(self, l, i):
        tr = self.tr
        c0 = i * T
        self.prenorm(l, 1, chained=False)
        tr.dma("pool", self.cos_sb[:], self.cosT[:, c0:c0 + T], w=["cos"], sem="cos")
        tr.dma("pool", self.sin_sb[:], self.sinT[:, c0:c0 + T], w=["sin"], sem="sin")
        slot, skey = self.wnext("QA")
        sv = slot[:, 0:2048].rearrange("p (kc c) -> p kc c", kc=8)
        for qc in range(2):
            for kc in range(8):
                tr.op("pe", lambda e, qc=qc, kc=kc: e.matmul(self.bank(qc), lhsT=sv[:, kc, qc * 128:(qc + 1) * 128],
                                                              rhs=self.hb[:, kc, :], start=(kc == 0), stop=(kc == 7)),
                      r=[skey, "hb"], w=["ps%d" % qc], sig=(kc == 7))
            tr.op("act", lambda e, qc=qc: e.activation(out=self.qaT[:, qc, :], in_=self.bank(qc), func=AF.Copy),
                  r=["ps%d" % qc], w=["qaT"])
        self.sumsq([(self.qaT[:, 0, :], 128, "qaT"), (self.qaT[:, 1, :], 128, "qaT")], self.bank(3), self.ones_bf, ["ps3"])
        self.rsqrt_bank(self.bank(3), "ps3", 4.0)
        for qc in range(2):
            tr.op("dve", lambda e, qc=qc: e.scalar_tensor_tensor(
                out=self.qnb[:, qc, :], in0=self.qaT[:, qc, :], scalar=self.vec(l, 56 + qc, 1), in1=self.rstd[:],
                op0=ALU.mult, op1=ALU.mult), r=["qaT", "rstd", "vecs"], w=["qnb"])
        tr.op("dve", lambda e: e.memset(self.qrT_hi, 0.0), r=["qnb"], w=["qrThi"])
        wq, wqkey = self.wnext("WQB")
        wqv = wq[:, 0:4096].rearrange("p (kc c) -> p kc c", kc=2)
        wkv, wkvkey = self.wnext("WKV")
        def stage_a(h):
            s = h % 2
            bn, br, bp = s, 2 + s, 4 + s
            for kc in range(2):
                tr.op("pe", lambda e, kc=kc: e.matmul(self.bank(bn), lhsT=wqv[:, kc, h * 192:h * 192 + 128],
                                                     rhs=self.qnb[:, kc, :], start=(kc == 0), stop=(kc == 1)),
                      r=[wqkey, "qnb"], w=["ps%d" % bn], sig=(kc == 1))
            for kc in range(2):
                tr.op("pe", lambda e, kc=kc: e.matmul(self.bank(br)[0:64, :], lhsT=wqv[:, kc, h * 192 + 128:h * 192 + 192],
                                                     rhs=self.qnb[:, kc, :], start=(kc == 0), stop=(kc == 1)),
                      r=[wqkey, "qnb"], w=["ps%d" % br], sig=(kc == 1))
            for kc in range(2):
                tr.op("pe", lambda e, kc=kc: e.matmul(self.bank(bp)[0:64, :], lhsT=wqv[:, kc, 1536 + h * 64:1536 + h * 64 + 64],
                                                     rhs=self.qnb[:, kc, :], start=(kc == 0), stop=(kc == 1)),
                      r=[wqkey, "qnb"], w=["ps%d" % bp], sig=(kc == 1))
            tr.op("act", lambda e: e.activation(out=self.qnope[:, s, :], in_=self.bank(bn), func=AF.Copy),
                  r=["ps%d" % bn], w=["qnope%d" % s])
            tr.op("dve", lambda e: e.tensor_tensor(out=self.rtmp[0:64, s, 0, :], in0=self.bank(br)[0:64, :], in1=self.cos_sb[:], op=ALU.mult),
                  r=["ps%d" % br, "cos"], w=["rtmp%da" % s], wait_w=["yt"])
            tr.op("dve", lambda e: e.tensor_tensor(out=self.rtmp[0:64, s, 1, :], in0=self.bank(bp)[0:64, :], in1=self.sin_sb[:], op=ALU.mult),
                  r=["ps%d" % bp, "sin"], w=["rtmp%db" % s], wait_w=["yt"])
            tr.op("dve", lambda e: e.tensor_tensor(out=self.qrT[:, h, :], in0=self.rtmp[0:64, s, 0, :], in1=self.rtmp[0:64, s, 1, :], op=ALU.add),
                  r=["rtmp%da" % s, "rtmp%db" % s, "yt"], w=["qrT%d" % h])

        def stage_b(h):
            s = h % 2
            tr.op("pe", lambda e: e.matmul(self.bank(6), lhsT=wkv[:, h * 128:(h + 1) * 128], rhs=self.qnope[:, s, :],
                                           start=True, stop=True), r=[wkvkey, "qnope%d" % s], w=["ps6"])
            tr.op("act", lambda e: e.activation(out=self.qlatT[:, h, :], in_=self.bank(6), func=AF.Copy),
                  r=["ps6"], w=["qlatT%d" % h])

        def stage_c(h):
            s = h % 2
            self.sumsq([(self.qlatT[:, h, :], 128, "qlatT%d" % h), (self.qrT[:, h, :], 64, "qrT%d" % h)],
                       self.bank(7), self.one1_bf, ["ps7"])
            c = 5 * s
            sm = lambda j: self.small[:, c + j:c + j + 1]
            tr.op("dve", lambda e: e.tensor_reduce(out=sm(0), in_=self.bank(7), axis=AX.X, op=ALU.max),
                  r=["ps7"], w=["small%d" % c])
            tr.op("dve", lambda e: e.tensor_tensor(out=sm(1), in0=sm(0), in1=self.kmax2[:, l:l + 1], op=ALU.mult),
                  r=["small%d" % c, "kmax2"], w=["small%d" % c])
            tr.op("act", lambda e: e.activation(out=sm(2), in_=sm(1), func=AF.Ln), r=["small%d" % c], w=["small%d" % c])
            tr.op("act", lambda e: e.activation(out=sm(3), in_=sm(2), func=AF.Exp, scale=0.5), r=["small%d" % c], w=["small%d" % c])
            tr.op("dve", lambda e: e.tensor_scalar(out=self.negM[:, h:h + 1], in0=sm(3), scalar1=-SCALE, scalar2=None,
                                                   op0=ALU.mult), r=["small%d" % c], w=["negM%d" % h])

        for h in range(NH + 2):
            if h < NH:
                stage_a(h)
            if 1 <= h <= NH:
                stage_b(h - 1)
            if h >= 2:
                stage_c(h - 2)
        W = T + 2 * PAD
        u_dram = self.u_s[l].rearrange("(kc p) s -> p kc s", p=128)

        def load_u(g):
            bsel = g % 2
            tr.dma("pool", self.ut[:, bsel], u_dram[:, 2 * g:2 * g + 2, c0:c0 + W],
                   r=["u_s%d" % l, "upad"], w=["ut%d" % bsel], sem="ld_u%d" % bsel)
            tr.dma("pool", self.invc_sb[:, bsel, :], self.invc[:, g, c0:c0 + T], w=["invc%d" % bsel], sem="ld_ic%d" % bsel)

        load_u(0)
        load_u(1)

        def win(g):
            bsel = g % 2
            utk, ick = "ut%d" % bsel, "invc%d" % bsel
            for cc in range(2):
                u = self.ut[:, bsel, cc, :]
                a, b = self.pw[:, 0, :], self.pw[:, 1, :]
                ka, kb = "pw0", "pw1"
                eng = "dve"
                tr.op(eng, lambda e, u=u, a=a: e.tensor_tensor(out=a[:, 1:W], in0=u[:, 0:W - 1], in1=u[:, 1:W], op=ALU.add),
                      r=[utk], w=[ka])
                cur, ck, oth, ok = a, ka, b, kb
                lo = 1
                hw = 1
                for lev in range(g):
                    nlo, nhi = lo + hw, W - hw
                    tr.op(eng, lambda e, cur=cur, oth=oth, nlo=nlo, nhi=nhi, hw=hw: e.tensor_tensor(
                        out=oth[:, nlo:nhi], in0=cur[:, nlo - hw:nhi - hw], in1=cur[:, nlo + hw:nhi + hw], op=ALU.add),
                        r=[ck], w=[ok])
                    cur, ck, oth, ok = oth, ok, cur, ck
                    lo = nlo
                    hw *= 2
                tr.op(eng, lambda e, cur=cur, oth=oth: e.tensor_tensor(out=oth[:, PAD:PAD + T], in0=cur[:, PAD:PAD + T],
                                                                      in1=self.invc_sb[:, bsel, :], op=ALU.mult),
                      r=[ck, ick], w=[ok])
                tr.op(eng, lambda e, oth=oth, u=u, cc=cc: e.tensor_tensor(out=self.pooled[:, cc, :], in0=oth[:, PAD:PAD + T],
                                                                           in1=u[:, PAD:PAD + T], op=ALU.subtract),
                      r=[ok, utk], w=["pooled%d" % cc])
            if g + 2 < 4:
                load_u(g + 2)

        def mm(g):
            if g % 2 == 0:
                wp, self._wpkey = self.wnext("WPOOL%d" % (g // 2))
                self._wpv = wp[:, 0:2048].rearrange("p (g kc m) -> p g kc m", g=4, kc=2)
                gp, self._gpkey = self.wnext("GP%d" % (g // 2))
                self._gpv = gp[:, 0:4096].rearrange("p (kc c) -> p kc c", kc=8)
            wpv, wpkey, gpv, gpkey = self._wpv, self._wpkey, self._gpv, self._gpkey
            self.dump("pooled_g%d" % g, self.pooled[:], ["pooled0", "pooled1"], BF16)
            later = []
            for j in range(2):
                ch = 2 * g + j
                for kc in range(2):
                    tr.op("pe", lambda e, j=j, kc=kc: e.matmul(self.bank(j), lhsT=wpv[:, g, kc, j * 128:(j + 1) * 128],
                                                               rhs=self.pooled[:, kc, :], start=(kc == 0), stop=(kc == 1)),
                          r=[wpkey, "pooled0", "pooled1"], w=["ps%d" % j], sig=(kc == 1))
                gc = (g % 2) * 2 + j
                for kc in range(8):
                    tr.op("pe", lambda e, gc=gc, kc=kc, j=j: e.matmul(self.bank(2 + j), lhsT=gpv[:, kc, gc * 128:(gc + 1) * 128],
                                                                     rhs=self.hb[:, kc, :], start=(kc == 0), stop=(kc == 7)),
                          r=[gpkey, "hb"], w=["ps%d" % (2 + j)], sig=(kc == 7))
                self.sigmoid(self.bank(2 + j), "ps%d" % (2 + j), j)
                later.append(lambda j=j, ch=ch: tr.op("dve", lambda e: e.scalar_tensor_tensor(
                    out=self.yt[:, ch, :], in0=self.bank(j), scalar=self.vec(l, 48 + ch, 1), in1=self.tmpf[:, j, :],
                    op0=ALU.mult, op1=ALU.mult), r=["ps%d" % j, "tmpf%d" % j, "vecs"], w=["yt"]))
            return later

        win(0)
        for g in range(4):
            later = mm(g)
            if g + 1 < 4:
                win(g + 1)
            for f in later:
                f()
        self.dump("pool", self.yt[:], ["yt"])
        self.dump("qlatT", self.qlatT, ["qlatT%d" % h for h in range(NH)], BF16)
        self.dump("qrT", self.qrT, ["qrT%d" % h for h in range(NH)], BF16)
        self.dump("negM", self.negM[:], ["negM%d" % h for h in range(NH)])
        self.attention(l, i)
        self.dump("merged", self.merged[:], ["merged"], BF16)
        for half in range(2):
            wo, wokey = self.wnext("WO%d" % half)
            wov = wo[:, 0:4096].rearrange("p (kc c) -> p kc c", kc=8)
            for cc in range(4):
                d = half * 4 + cc
                bk = d % 6
                for kc in range(8):
                    tr.op("pe", lambda e, bk=bk, cc=cc, kc=kc, wov=wov: e.matmul(
                        self.bank(bk), lhsT=wov[:, kc, cc * 128:(cc + 1) * 128], rhs=self.merged[:, kc, :],
                        start=(kc == 0), stop=(kc == 7)), r=[wokey, "merged"], w=["ps%d" % bk], sig=(kc == 7))
                tr.op("act", lambda e, bk=bk, d=d: e.activation(out=self.yt[:, d, :], in_=self.bank(bk), func=AF.Copy),
                      r=["ps%d" % bk], w=["yt"])
                if USE_B:
                    self.post_sq(d, bk)
                    if d >= 1:
                        self.post_mm(d - 1)
        if USE_B:
            self.post_mm(7)
        self.postnorm(l, 1, stats_done=USE_B)
        self.dump("x2", self.xt[:], ["xt"])

    def sigmoid(self, src, skey, s):
        tr = self.tr
        k = "tmpf%d" % s
        tr.op("act", lambda e: e.activation(out=self.tmpf[:, s, :], in_=src, func=AF.Exp, scale=-1.0), r=[skey], w=[k])
        tr.op("act", lambda e: e.activation(out=self.tmpf[:, s, :], in_=self.tmpf[:, s, :], func=AF.Ln, bias=self.oneb[:, 0:1], scale=1.0),
              r=[k, "oneb"], w=[k])
        tr.op("act", lambda e: e.activation(out=self.tmpf[:, s, :], in_=self.tmpf[:, s, :], func=AF.Exp, scale=-1.0), r=[k], w=[k])

    def attention(self, l, i):
        tr = self.tr
        NP = self.NKT // 2
        NS = NH * NP
        AHEAD = 2

        def ptile(sidx):
            j = sidx % 3
            return self.PTs[:, j * 2 * T:(j + 1) * 2 * T], "PT%d" % j

        def qk(sidx):
            h, pp = divmod(sidx, NP)
            b = (sidx % 3) * 2
            for t2 in range(2):
                kt = pp * 2 + t2
                extra = ["PT%d" % ((sidx - AHEAD) % 3)] if (t2 == 0 and sidx >= AHEAD) else []
                tr.op("pe", lambda e, kt=kt, t2=t2: e.matmul(self.bank(b + t2), lhsT=self.Kc[:, kt * 128:(kt + 1) * 128],
                                                            rhs=self.qlatT[:, h, :], start=True, stop=False),
                      r=["Kc", "qlatT%d" % h] + extra, w=["ps%d" % (b + t2)], sig=False)
                tr.op("pe", lambda e, kt=kt, t2=t2: e.matmul(self.bank(b + t2), lhsT=self.Kr[:, kt * 128:(kt + 1) * 128],
                                                            rhs=self.qrT_full[:, h, :], start=False, stop=True),
                      r=["Kr", "Krz", "qrT%d" % h, "qrThi"], w=["ps%d" % (b + t2)], sig=(t2 == 1))
            pt, ptk = ptile(sidx)
            tr.op("act", lambda e: e.activation(out=pt, in_=self.bank(b, 2), func=AF.Exp, bias=self.negM[:, h:h + 1], scale=SCALE),
                  r=["ps%d" % b, "ps%d" % (b + 1), "negM%d" % h], w=[ptk])

        def pv(sidx):
            h, pp = divmod(sidx, NP)
            pt, ptk = ptile(sidx)
            for t2 in range(2):
                kt = pp * 2 + t2
                tr.op("pe", lambda e, kt=kt, t2=t2: e.matmul(self.bank(6), lhsT=self.Kt[:, kt * 128:(kt + 1) * 128],
                                                            rhs=pt[:, t2 * T:(t2 + 1) * T],
                                                            start=(kt == 0), stop=(kt == self.NKT - 1)),
                      r=["Kt", ptk], w=["ps6"], sig=(t2 == 1))
            acc = self.accd[:, (h % 2) * T:(h % 2 + 1) * T]
            acck = "accd%d" % (h % 2)
            if pp == 0:
                tr.op("dve", lambda e: e.tensor_tensor(out=acc, in0=pt[:, 0:T], in1=pt[:, T:2 * T], op=ALU.add), r=[ptk], w=[acck])
            else:
                tr.op("dve", lambda e: e.tensor_tensor(out=self.ptsum[:], in0=pt[:, 0:T], in1=pt[:, T:2 * T], op=ALU.add),
                      r=[ptk], w=["ptsum"])
                tr.op("dve", lambda e: e.tensor_tensor(out=acc, in0=acc, in1=self.ptsum[:], op=ALU.add), r=["ptsum", acck], w=[acck])

        def ep0(h):
            tr.op("dve", lambda e: e.tensor_copy(out=self.ovs[:], in_=self.bank(6)), r=["ps6"], w=["ovs"])

        def ep1(h):
            acc = self.accd[:, (h % 2) * T:(h % 2 + 1) * T]
            tr.op("pe", lambda e: e.matmul(self.bank(7), lhsT=self.ones_f[:], rhs=acc, start=True, stop=True),
                  r=["accd%d" % (h % 2), "ones_f"], w=["ps7"])
            tr.op("act", lambda e: e.activation(out=self.rstd[:], in_=self.bank(7), func=AF.Ln), r=["ps7"], w=["rstd"])
            tr.op("act", lambda e: e.activation(out=self.rstd[:], in_=self.rstd[:], func=AF.Exp, scale=-1.0), r=["rstd"], w=["rstd"])
            tr.op("dve", lambda e: e.tensor_tensor(out=self.olat[:], in0=self.ovs[:], in1=self.rstd[:], op=ALU.mult),
                  r=["ovs", "rstd"], w=["olat"])

        def gates(h):
            if h % 4 == 0:
                ga, self._gakey = self.wnext("GA%d" % (h // 4))
                self._gav = ga[:, 0:4096].rearrange("p (kc c) -> p kc c", kc=8)
            gav, gakey = self._gav, self._gakey
            gc = h % 4
            for kc in range(8):
                tr.op("pe", lambda e, kc=kc: e.matmul(self.bank(7), lhsT=gav[:, kc, gc * 128:(gc + 1) * 128], rhs=self.hb[:, kc, :],
                                                     start=(kc == 0), stop=(kc == 7)), r=[gakey, "hb"], w=["ps7"], sig=(kc == 7))
            self.sigmoid(self.bank(7), "ps7", 0)

        def ep2(h):
            tr.op("pe", lambda e: e.matmul(self.bank(7), lhsT=self.wv_sb[:, h * 128:(h + 1) * 128], rhs=self.olat[:],
                                           start=True, stop=True), r=["wv_sb", "olat"], w=["ps7"])
            tr.op("dve", lambda e: e.tensor_tensor(out=self.tmpf[:, 1, :], in0=self.bank(7), in1=self.tmpf[:, 0, :], op=ALU.mult),
                  r=["ps7", "tmpf0"], w=["tmpf1"])
            tr.op("dve", lambda e: e.tensor_tensor(out=self.merged[:, h, :], in0=self.tmpf[:, 1, :], in1=self.yt[:, h, :], op=ALU.add),
                  r=["tmpf1", "yt"], w=["merged"])

        fns = {0: ep0, 1: ep1, 2: ep2, 3: gates}
        todo = []
        off2 = max(1, min(5, NP // 2 - 1))
        for sidx in range(min(AHEAD, NS)):
            qk(sidx)
        for sidx in range(NS):
            h, pp = divmod(sidx, NP)
            if pp == 0:
                todo.append((sidx + NP // 2, 3, h))
            if sidx + AHEAD < NS:
                qk(sidx + AHEAD)
            pv(sidx)
            if pp == NP - 1:
                ep0(h)
                todo.append((sidx + 1, 1, h))
                todo.append((sidx + off2, 2, h))
            while todo and todo[0][0] <= sidx:
                _, kind, hh = todo.pop(0)
                fns[kind](hh)
        for _, kind, hh in todo:
            fns[kind](hh)

    def build(self):
        tr, L, NT = self.tr, self.L, self.NT
        for i in range(NT):
            self.plan(self.plan_ffn(0, 1) + self.plan_kside(0))
        for l in range(L):
            for i in range(NT):
                self.plan(self.plan_mixer(l) + self.plan_ffn(l, 2))
                if l < L - 1:
                    self.plan(self.plan_ffn(l + 1, 1) + self.plan_kside(l + 1))
        self.prologue()
        xTv = self.xT.rearrange("(kc p) s -> p kc s", p=128)
        x1v = self.x1_s.rearrange("(kc p) s -> p kc s", p=128)
        yTv = self.yT.rearrange("(kc p) s -> p kc s", p=128)
        for i in range(NT):
            tr.dma("pool", self.xt[:], xTv[:, :, i * T:(i + 1) * T], w=["xt"], sem="ld_x")
            self.ffn(0, 1, chained=False)
            tr.dma("pool", x1v[:, :, i * T:(i + 1) * T], self.xt[:], r=["xt"], w=["x1_s"], sem="st_x")
            self.kside(0, i)
            if i == 0:
                for lc in range(1, L):
                    self._cast_layer(lc)
        for l in range(L):
            self.load_kside(l)
            for i in range(NT):
                tr.dma("pool", self.xt[:], x1v[:, :, i * T:(i + 1) * T], r=["x1_s"], w=["xt"], sem="ld_x")
                self.mixer(l, i)
                self.ffn(l, 2)
                if l < L - 1:
                    self.ffn(l + 1, 1)
                    tr.dma("pool", x1v[:, :, i * T:(i + 1) * T], self.xt[:], r=["xt"], w=["x1_s"], sem="st_x")
                    self.kside(l + 1, i)
                else:
                    tr.dma("pool", yTv[:, :, i * T:(i + 1) * T], self.xt[:], r=["xt"], w=["yT"], sem="st_y")
        tr.wait_all("pool", ["yT", "x1_s"] + ["dbg_" + n for n in self._dumped])
        assert self.wpos == len(self.wplan)
        return self.nc


def _rope_tables(S):
    inv = (1.0 / (np.float32(10000.0) ** (np.arange(0, 64, 2, dtype=np.float32) / np.float32(64)))).astype(np.float32)
    ang = np.arange(S, dtype=np.float32)[:, None] * inv[None, :]
    ang = np.concatenate([ang, ang], axis=-1).astype(np.float32)
    cos = np.cos(ang).astype(np.float32)
    sin = np.sin(ang).astype(np.float32)
    sgn = np.concatenate([-np.ones(32, np.float32), np.ones(32, np.float32)])
    return np.ascontiguousarray(cos.T), np.ascontiguousarray((sin * sgn[None, :]).T)


def _invcnt(S):
    t = np.arange(S)[:, None]
    half = np.array([1, 2, 4, 8])[None, :]
    lo = np.clip(t - half, 0, S)
    hi = np.clip(t + half, 0, S)
    ic = (1.0 / (hi - lo).astype(np.float32)).astype(np.float32)
    return np.ascontiguousarray(np.broadcast_to(ic.T[None, :, :], (128, 4, S)))


def host_inputs(S, L, x_seq, c_seq, P):
    fm = lambda v: np.ascontiguousarray(np.asarray(v, np.float32).reshape(-1, 128).T)
    cols = []
    for l in range(L):
        cols += [fm(P["n1_pre"][l]), fm(P["n1_post"][l]), fm(P["nm_pre"][l]), fm(P["nm_post"][l]),
                 fm(P["n2_pre"][l]), fm(P["n2_post"][l]), fm(P["pool_scale"][l]), fm(P["q_a_norm"][l]),
                 fm(P["kv_a_norm"][l]), fm(P["b_ada"][l])]
    vecs = np.ascontiguousarray(np.concatenate(cols, axis=1))
    assert vecs.shape == (128, L * NV)
    wk = np.asarray(P["w_kvb"][:L], np.float32).reshape(L, 128, 8, 256)[:, :, :, 0:128]
    wkT = np.ascontiguousarray(wk.transpose(0, 3, 2, 1)).reshape(L, 128, 1024)
    cosT, sinT = _rope_tables(S)
    f = lambda k: np.ascontiguousarray(np.asarray(P[k][:L], np.float32))
    return {
        "xT": np.ascontiguousarray(np.asarray(x_seq, np.float32).T),
        "cT": fm(c_seq),
        "vecs": vecs,
        "w_ada": f("w_ada"), "w1_gu": f("w1_gu"), "w1_down": f("w1_down"), "w2_gu": f("w2_gu"), "w2_down": f("w2_down"),
        "w_in": f("w_in"), "w_qb": f("w_qb"), "w_kvb": f("w_kvb"), "wkT": wkT, "w_pool": f("w_pool"), "w_out": f("w_out"),
        "cosT": cosT, "sinT": sinT, "invc": _invcnt(S), "ident": np.eye(128, dtype=np.float32),
    }


_WKEYS = ["w_ada", "b_ada", "n1_pre", "w1_gu", "w1_down", "n1_post", "nm_pre", "w_in", "q_a_norm", "w_qb", "kv_a_norm",
          "w_kvb", "w_pool", "pool_scale", "w_out", "nm_post", "n2_pre", "w2_gu", "w2_down", "n2_post"]


def kernel(x_prompt, x_sample, c_prompt, c_sample, **P):
    x_prompt = np.asarray(x_prompt); x_sample = np.asarray(x_sample)
    c_prompt = np.asarray(c_prompt); c_sample = np.asarray(c_sample)
    S = x_prompt.shape[1]
    L = np.asarray(P["w_ada"]).shape[0]
    P = {k: np.asarray(v) for k, v in P.items()}
    seqs = [(x_prompt[b], c_prompt[b]) for b in range(x_prompt.shape[0])] + \
           [(x_sample[b], c_sample[b]) for b in range(x_sample.shape[0])]
    n_real = len(seqs)
    while len(seqs) < 8:
        seqs.append(seqs[len(seqs) - n_real])
    base = host_inputs(S, L, seqs[0][0], seqs[0][1], P)
    in_maps = []
    fm = lambda v: np.ascontiguousarray(np.asarray(v, np.float32).reshape(-1, 128).T)
    for (xs, cs) in seqs[:8]:
        m = dict(base)
        m["xT"] = np.ascontiguousarray(np.asarray(xs, np.float32).T)
        m["cT"] = fm(cs)
        in_maps.append(m)
    nc = Builder(S, L).build()
    res = run_bass_kernel_spmd(nc, in_maps, core_ids=list(range(8)))
    outs = [np.ascontiguousarray(res.results[c]["yT"].T).astype(np.float32) for c in range(n_real)]
    nb = x_prompt.shape[0]
    return (np.stack(outs[:nb], 0), np.stack(outs[nb:], 0))
```

```python
import numpy as np
from contextlib import ExitStack
import concourse.bass as bass
import concourse.mybir as mybir
from concourse.bass_utils import run_bass_kernel_spmd

F32 = mybir.dt.float32
BF16 = mybir.dt.bfloat16
AF = mybir.ActivationFunctionType
ALU = mybir.AluOpType
AX = mybir.AxisListType

D = 1024
DFF = 2816
NH = 8
T = 512
NV = 131
EPS = 1e-6
SCALE = 192.0 ** -0.5
RING = 4
LIVE = 2
PAD = 16
USE_B = True
USE_B_FFN = True


class Tracker:
    def __init__(self, nc, es):
        self.nc = nc
        self.es = es
        self.eng = {"pe": nc.tensor, "act": nc.scalar, "dve": nc.vector, "pool": nc.gpsimd, "sp": nc.sync}
        self.sem = {}
        self.cnt = {}
        for e in ("pe", "act", "dve", "pool"):
            self.sem[e] = es.enter_context(nc.semaphore("s_" + e))
            self.cnt[e] = 0
        self.waited = {}
        self.last_w = {}
        self.readers = {}
        self.n_ins = 0

    def dsem(self, name):
        k = "d:" + name
        if k not in self.sem:
            self.sem[k] = self.es.enter_context(self.nc.semaphore("d_" + name))
            self.cnt[k] = 0
        return k

    def _deps(self, r, w):
        deps = {}

        def add(tok):
            if tok is None:
                return
            k, v = tok
            if deps.get(k, 0) < v:
                deps[k] = v

        for key in r:
            add(self.last_w.get(key))
        for key in w:
            add(self.last_w.get(key))
            for tok in self.readers.get(key, {}).items():
                add(tok)
        return deps

    def _wait(self, e, deps, keep_last=False):
        eng = self.eng[e]
        todo = []
        for k, v in deps.items():
            if k == e and (e == "pe" or v <= self.cnt[e] - 2):
                continue
            if self.waited.get((e, k), 0) >= v:
                continue
            if k.startswith("d:"):
                assert self.cnt[k] == v, ("dma sem ambiguity", k, v, self.cnt[k])
            todo.append((k, v))
            self.waited[(e, k)] = v
        last = None
        if keep_last and todo:
            last = todo.pop()
        for k, v in todo:
            eng.wait_ge(self.sem[k], v)
            self.n_ins += 1
        return last

    def _record(self, tok, r, w):
        for key in r:
            self.readers.setdefault(key, {})
            d = self.readers[key]
            if d.get(tok[0], 0) < tok[1]:
                d[tok[0]] = tok[1]
        for key in w:
            self.last_w[key] = tok
            self.readers[key] = {}

    def op(self, e, fn, r=(), w=(), sig=True, wait_w=()):
        last = self._wait(e, self._deps(r, list(w) + list(wait_w)), keep_last=True)
        ins = fn(self.eng[e])
        if last is not None:
            ins._wait_ge(self.sem[last[0]], last[1])
        self.n_ins += 1
        if sig:
            self.cnt[e] += 1
            ins.then_inc(self.sem[e], 1)
            tok = (e, self.cnt[e])
        else:
            tok = (e, self.cnt[e] + 1)
        self._record(tok, r, w)
        return tok

    def dma(self, q, out, in_, r=(), w=(), sem="x", **kw):
        k = self.dsem(sem)
        self._wait(q, self._deps(r, w))
        ins = self.eng[q].dma_start(out=out, in_=in_, **kw)
        self.n_ins += 1
        self.cnt[k] += 16
        ins.then_inc(self.sem[k], 16)
        tok = (k, self.cnt[k])
        self._record(tok, r, w)
        return tok

    def wait_all(self, e, keys):
        self._wait(e, self._deps(keys, keys))


class Builder:
    def __init__(self, S, L, debug=False):
        assert S % T == 0
        self.S, self.L = S, L
        self.debug = debug
        self._dumped = set()
        self.NT = S // T
        self.NKT = S // 128
        self.nc = bass.Bass("TRN2", target_bir_lowering=False)
        self.es = ExitStack()
        self.tr = Tracker(self.nc, self.es)
        self._declare()

    def _dram(self, name, shape, dt, kind):
        return self.nc.dram_tensor(name, list(shape), dt, kind=kind).ap()

    def _sb(self, name, shape, dt):
        return self.es.enter_context(self.nc.sbuf_tensor(name, list(shape), dt))

    def _declare(self):
        S, L = self.S, self.L
        I = "ExternalInput"
        self.xT = self._dram("xT", [D, S], F32, I)
        self.cT = self._dram("cT", [128, 8], F32, I)
        self.vecs = self._dram("vecs", [128, L * NV], F32, I)
        self.w_ada = self._dram("w_ada", [L, D, 9 * D], F32, I)
        self.w1_gu = self._dram("w1_gu", [L, D, 2 * DFF], F32, I)
        self.w1_down = self._dram("w1_down", [L, DFF, D], F32, I)
        self.w2_gu = self._dram("w2_gu", [L, D, 2 * DFF], F32, I)
        self.w2_down = self._dram("w2_down", [L, DFF, D], F32, I)
        self.w_in = self._dram("w_in", [L, D, 3520], F32, I)
        self.w_qb = self._dram("w_qb", [L, 256, 1536], F32, I)
        self.w_kvb = self._dram("w_kvb", [L, 128, 2048], F32, I)
        self.wkT = self._dram("wkT", [L, 128, 1024], F32, I)
        self.w_pool = self._dram("w_pool", [L, 4, 256, 256], F32, I)
        self.w_out = self._dram("w_out", [L, D, D], F32, I)
        self.cosT = self._dram("cosT", [64, S], F32, I)
        self.sinT = self._dram("sinT", [64, S], F32, I)
        self.invc = self._dram("invc", [128, 4, S], F32, I)
        self.ident = self._dram("ident", [128, 128], F32, I)
        self.yT = self._dram("yT", [D, S], F32, "ExternalOutput")
        N = "ExternalOutput" if self.debug else "Internal"
        E = 4096
        self.s_gu1 = self._dram("s_gu1", [L, 11, 128, E], BF16, N)
        self.s_gu2 = self._dram("s_gu2", [L, 11, 128, E], BF16, N)
        self.s_d1 = self._dram("s_d1", [L, 8, 128, 22 * 128], BF16, N)
        self.s_d2 = self._dram("s_d2", [L, 8, 128, 22 * 128], BF16, N)
        self.s_in = self._dram("s_in", [L, 8, 128, E], BF16, N)
        self.s_qb = self._dram("s_qb", [L, 128, E], BF16, N)
        self.s_kv = self._dram("s_kv", [L, 128, 2048], BF16, N)
        self.s_pool = self._dram("s_pool", [L, 128, 2048], BF16, N)
        self.s_out = self._dram("s_out", [L, 2, 128, E], BF16, N)
        self.x1_s = self._dram("x1_s", [D, S], F32, N)
        self.u_s = self._dram("u_s", [L, D, S + 2 * PAD], F32, N)
        self.cT_s = self._dram("cT_s", [L, 128, S], BF16, N)
        self.kr_s = self._dram("kr_s", [L, 64, S], BF16, N)
        self.ctok_s = self._dram("ctok_s", [L, S, 128], BF16, N)

        sb = self._sb
        self.ones_bf = sb("ones_bf", [128, 128], BF16)
        self.one1_bf = sb("one1_bf", [128, 128], BF16)
        self.ones_f = sb("ones_f", [128, 128], F32)
        self.ident_bf = sb("ident_bf", [128, 128], BF16)
        self.vecs_sb = sb("vecs_sb", [128, L * NV], F32)
        self.mod = sb("mod", [128, L * 72], F32)
        self.coef = sb("coef", [128, L * 72], F32)
        self.cT_sb = sb("cT_sb", [128, 8], F32)
        self.sc_sb = sb("sc_sb", [128, 8], F32)
        self.kmax2 = sb("kmax2", [128, L], F32)
        self.small = sb("small", [128, 16], F32)
        self.negM = sb("negM", [128, 8], F32)
        self.epsb = sb("epsb", [128, 1], F32)
        self.oneb = sb("oneb", [128, 1], F32)
        self.Kc = sb("Kc", [128, S], BF16)
        self.Kt = sb("Kt", [128, S], BF16)
        self.Kr = sb("Kr", [128, S], BF16)
        self.xt = sb("xt", [128, 8, T], F32)
        self.hb = sb("hb", [128, 8, T], BF16)
        self.sq = sb("sq", [128, 2, T], BF16)
        self.rstd = sb("rstd", [128, T], F32)
        self.tmpf = sb("tmpf", [128, 2, T], F32)
        self.yt = sb("yt", [128, 8, T], F32)
        self.ring = [sb("ring%d" % i, [128, 4096], BF16) for i in range(RING)]
        self.regA = sb("regA", [128, 12288], BF16)
        self.qaT = sb("qaT", [128, 2, T], F32)
        self.ut = sb("ut", [128, 2, 2, T + 2 * PAD], F32)
        self.pw = sb("pw", [128, 2, T + 2 * PAD], F32)
        self.ovs = sb("ovs", [128, T], F32)
        self.invc_sb = sb("invc_sb", [128, 2, T], F32)
        self.PTs = sb("PTs", [128, 6 * T], BF16)
        self.accd = sb("accd", [128, 2 * T], F32)
        self.olat = sb("olat", [128, T], BF16)
        self.ptsum = sb("ptsum", [128, T], BF16)
        self.merged = sb("merged", [128, 8, T], BF16)
        self.wv_sb = sb("wv_sb", [128, 1024], BF16)
        self.cos_sb = sb("cos_sb", [64, T], F32)
        self.sin_sb = sb("sin_sb", [64, T], F32)
        self.ps = self.es.enter_context(self.nc.psum_tensor("ps", [128, 8 * T], F32))
        A = self.regA
        self.aT = A[:, 0:22 * T].rearrange("p (f t) -> p f t", t=T)
        self.qlatT = A[:, 0:8 * T].rearrange("p (h t) -> p h t", t=T)
        self.qrT = A[0:64, 8 * T:16 * T].rearrange("p (h t) -> p h t", t=T)
        self.qrT_full = A[:, 8 * T:16 * T].rearrange("p (h t) -> p h t", t=T)
        self.qrT_hi = A[64:128, 8 * T:16 * T]
        self.qnb = A[:, 16 * T:18 * T].rearrange("p (k t) -> p k t", t=T)
        self.qnope = A[:, 18 * T:20 * T].rearrange("p (k t) -> p k t", t=T)
        self.krt = A[0:64, 20 * T:21 * T]
        self.ctile = A[:, 21 * T:22 * T]
        self.ctk = A[:, 22 * T:23 * T]
        self.rtmp = self.yt[:, 0:4, :].rearrange("p (s two) t -> p s two t", two=2)
        self.pooled = self.ovs[:].bitcast(BF16).rearrange("p (a t) -> p a t", a=2)
        self.ytb = self.yt[:].rearrange("p k t -> p (k t)").bitcast(BF16)[:, 0:8 * T].rearrange("p (k t) -> p k t", t=T)
        self.wpos = 0
        self.wplan = []

    def dump(self, name, ap, keys, dt=F32):
        if not self.debug or name in self._dumped:
            return
        self._dumped.add(name)
        d = self._dram("dbg_" + name, list(ap.shape), dt, "ExternalOutput")
        self.tr.dma("pool", d, ap, r=keys, w=["dbg_" + name], sem="dbg_" + name)

    def bank(self, b, n=1):
        return self.ps[:, b * T:(b + n) * T]

    def coefv(self, l, which):
        return self.coef[:, l * 72 + which * 8: l * 72 + which * 8 + 8]

    def vec(self, l, off, n=8):
        return self.vecs_sb[:, l * NV + off: l * NV + off + n]

    def plan(self, items):
        self.wplan.extend(items)

    def _issue(self, pos):
        if pos >= len(self.wplan):
            return
        name, ap, ne, lay = self.wplan[pos]
        s = pos % RING
        self.tr.dma("sp", self.ring[s][:, 0:ne], ap, r=["wcast%d" % lay], w=["ring%d" % s], sem="ring%d" % s)

    def wnext(self, name):
        pos = self.wpos
        assert self.wplan[pos][0] == name, (self.wplan[pos][0], name)
        if pos == 0:
            for p in range(RING - LIVE):
                self._issue(p)
        self._issue(pos + RING - LIVE)
        self.wpos += 1
        s = pos % RING
        return self.ring[s], "ring%d" % s

    def prologue(self):
        tr, L = self.tr, self.L
        g = "pool"
        tr.op("dve", lambda e: e.memset(self.ones_bf[:], 1.0 / 1024.0), w=["ones_bf"])
        tr.op("dve", lambda e: e.memset(self.one1_bf[:], 1.0), w=["one1_bf"])
        tr.op("dve", lambda e: e.memset(self.ones_f[:], 1.0), w=["ones_f"])
        tr.op("dve", lambda e: e.memset(self.kmax2[:], 0.0), w=["kmax2"])
        tr.op("dve", lambda e: e.memset(self.epsb[:], EPS), w=["epsb"])
        tr.op("dve", lambda e: e.memset(self.Kr[64:128, :], 0.0), w=["Krz"])
        tr.op("dve", lambda e: e.memset(self.oneb[:], 1.0), w=["oneb"])
        tr.dma(g, self.ident_bf[:], self.ident[:, :], w=["ident_bf"], sem="c0")
        tr.dma("sp", self.vecs_sb[:], self.vecs[:, :], w=["vecs"], sem="c1")
        tr.dma("sp", self.cT_sb[:], self.cT[:, :], w=["cT"], sem="c2")
        tr.op("dve", lambda e: e.memset(self.pw[:, 0, 0:PAD], 0.0), w=["pw"])
        for l in range(L):
            for side in (0, 1):
                c0 = 0 if side == 0 else self.S + PAD
                tr.dma(g, self.u_s[l].rearrange("(kc p) s -> p kc s", p=128)[:, :, c0:c0 + PAD],
                       self.pw[:, 0:1, 0:PAD].to_broadcast([128, 8, PAD]), r=["pw"], w=["upad"], sem="upad")
        for l in range(L):
            self._cast_layer(l)
        tr.op("act", lambda e: e.activation(out=self.sc_sb[:], in_=self.cT_sb[:], func=AF.Silu),
              r=["cT"], w=["sc"])
        stage = [self.xt, self.yt]
        rows = [(self.rstd[0:1, :], "rstd"), (self.tmpf[0:1, 0, :], "tmpf0")]
        blk = 0
        for l in range(L):
            for cb in range(18):
                st = stage[blk % 2]
                key = "xt" if blk % 2 == 0 else "yt"
                stv = st[:].rearrange("p k t -> p (k t)")[:, 0:4096].rearrange("p (k c) -> p k c", k=8)
                tr.dma("sp", stv, self.w_ada[l].rearrange("(kc p) f -> p kc f", p=128)[:, :, cb * 512:(cb + 1) * 512],
                       w=[key], sem="stage%d" % (blk % 2))
                br = blk % 6
                for kc in range(8):
                    tr.op("pe", lambda e, kc=kc, br=br, stv=stv: e.matmul(
                        self.bank(br)[0:1, :], lhsT=self.sc_sb[:, kc:kc + 1], rhs=stv[:, kc, :],
                        start=(kc == 0), stop=(kc == 7)), r=[key, "sc"], w=["ps%d" % br], sig=(kc == 7))
                rowap, rowk = rows[blk % 2]
                tr.op("act", lambda e, br=br, rowap=rowap: e.activation(out=rowap, in_=self.bank(br)[0:1, :], func=AF.Copy),
                      r=["ps%d" % br], w=[rowk])
                for jj in range(4):
                    j = cb * 4 + jj
                    tr.op("pe", lambda e, jj=jj, j=j, rowap=rowap: e.matmul(
                        self.bank(7)[:, j:j + 1], lhsT=rowap[0:1, jj * 128:(jj + 1) * 128], rhs=self.ones_f[0:1, 0:1],
                        start=True, stop=True), r=[rowk, "ones_f"], w=["ps7"], sig=(jj == 3))
                blk += 1
            tr.op("dve", lambda e, l=l: e.tensor_tensor(out=self.mod[:, l * 72:(l + 1) * 72], in0=self.bank(7)[:, 0:72],
                                                        in1=self.vec(l, 59, 72), op=ALU.add),
                  r=["ps7", "vecs"], w=["mod"])
            for si, (pre, post, half) in enumerate(((0, 8, 0.5), (16, 24, 1.0), (32, 40, 0.5))):
                m = lambda i, l=l: self.mod[:, l * 72 + i * 8: l * 72 + i * 8 + 8]
                tr.op("dve", lambda e, si=si, pre=pre, l=l, m=m: e.scalar_tensor_tensor(
                    out=self.coefv(l, si * 3 + 0), in0=m(si * 3 + 1), scalar=1.0, in1=self.vec(l, pre),
                    op0=ALU.add, op1=ALU.mult), r=["mod", "vecs"], w=["coef"])
                tr.op("dve", lambda e, si=si, l=l, m=m: e.tensor_copy(out=self.coefv(l, si * 3 + 1), in_=m(si * 3 + 0)),
                      r=["mod"], w=["coef"])
                tr.op("dve", lambda e, si=si, post=post, half=half, l=l, m=m: e.scalar_tensor_tensor(
                    out=self.coefv(l, si * 3 + 2), in0=m(si * 3 + 2), scalar=half, in1=self.vec(l, post),
                    op0=ALU.mult, op1=ALU.mult), r=["mod", "vecs"], w=["coef"])

    def _cast(self, out, in_):
        self.tr.dma("pool", out, in_, w=["wcast%d" % self._cl], sem="wcast%d" % self._cl)

    def _cast_layer(self, l):
        self._cl = l
        def gu(dst, src):
            for j in range(11):
                for two in range(2):
                    self._cast(dst[l, j].rearrange("p (kc c) -> p kc c", kc=8)[:, :, two * 256:(two + 1) * 256],
                               src[l].rearrange("(kc p) f -> p kc f", p=128)[:, :, two * DFF + j * 256: two * DFF + (j + 1) * 256])

        def down(dst, src):
            for d in range(8):
                self._cast(dst[l, d].rearrange("p (fc m) -> p fc m", m=128),
                           src[l].rearrange("(fc p) d -> p fc d", p=128)[:, :, d * 128:(d + 1) * 128])

        win = self.w_in[l].rearrange("(kc p) f -> p kc f", p=128)

        def inslot(si, c0, n, o=0, width=512):
            self._cast(self.s_in[l, si][:, 0:8 * width].rearrange("p (kc c) -> p kc c", kc=8)[:, :, o:o + n], win[:, :, c0:c0 + n])

        gu(self.s_gu1, self.w1_gu)
        down(self.s_d1, self.w1_down)
        inslot(0, 0, 256, width=256)
        inslot(1, 256, 192, width=256)
        inslot(1, 416, 32, 192, width=256)
        inslot(1, 384, 32, 224, width=256)
        inslot(2, 448, 512); inslot(3, 960, 512)
        inslot(4, 1472, 512); inslot(5, 1984, 512)
        inslot(6, 2496, 512); inslot(7, 3008, 512)
        qb = self.s_qb[l].rearrange("p (kc c) -> p kc c", kc=2)
        wq = self.w_qb[l].rearrange("(kc p) f -> p kc f", p=128)
        self._cast(qb[:, :, 0:1536], wq)
        qbp = qb[:, :, 1536:2048].rearrange("p k (h e) -> p k h e", e=64)
        wqh = wq.rearrange("p k (h e) -> p k h e", e=192)
        for kc in range(2):
            self._cast(qbp[:, kc, :, 0:32], wqh[:, kc, :, 160:192])
            self._cast(qbp[:, kc, :, 32:64], wqh[:, kc, :, 128:160])
        self._cast(self.s_kv[l][:, 0:1024], self.wkT[l])
        self._cast(self.s_kv[l][:, 1024:2048].rearrange("p (h e) -> p h e", e=128),
                   self.w_kvb[l].rearrange("p (h e) -> p h e", e=256)[:, :, 128:256])
        self._cast(self.s_pool[l].rearrange("p (g kc m) -> p g kc m", g=4, kc=2),
                   self.w_pool[l].rearrange("g (kc p) m -> p g kc m", p=128))
        for s in range(2):
            self._cast(self.s_out[l, s].rearrange("p (kc c) -> p kc c", kc=8),
                       self.w_out[l].rearrange("(kc p) f -> p kc f", p=128)[:, :, s * 512:(s + 1) * 512])
        gu(self.s_gu2, self.w2_gu)
        down(self.s_d2, self.w2_down)

    def plan_ffn(self, l, which):
        gu = self.s_gu1 if which == 1 else self.s_gu2
        dn = self.s_d1 if which == 1 else self.s_d2
        return ([("gu%d_%d" % (which, j), gu[l, j], 4096, l) for j in range(11)]
                + [("dn%d_%d" % (which, d), dn[l, d], 22 * 128, l) for d in range(8)])

    def plan_kside(self, l):
        return [("K1", self.s_in[l, 1][:, 0:2048], 2048, l), ("U0", self.s_in[l, 2][:, :], 4096, l), ("U1", self.s_in[l, 3][:, :], 4096, l)]

    def plan_mixer(self, l):
        return [("QA", self.s_in[l, 0][:, 0:2048], 2048, l), ("WQB", self.s_qb[l][:, :], 4096, l), ("WKV", self.s_kv[l][:, :], 2048, l),
                ("WPOOL0", self.s_pool[l][:, :], 2048, l), ("GP0", self.s_in[l, 6][:, :], 4096, l),
                ("WPOOL1", self.s_pool[l][:, :], 2048, l), ("GP1", self.s_in[l, 7][:, :], 4096, l),
                ("GA0", self.s_in[l, 4][:, :], 4096, l), ("GA1", self.s_in[l, 5][:, :], 4096, l),
                ("WO0", self.s_out[l, 0][:, :], 4096, l), ("WO1", self.s_out[l, 1][:, :], 4096, l)]

    def sumsq(self, srcs, bank_ap, ones, keys_r, wide=None, sq_eng="alt"):
        tr = self.tr
        n = len(srcs)
        for i, (ap, rows, key) in enumerate(srcs):
            s = i % 2
            if wide is not None:
                dst, dkey, ww = wide[0][0:rows, i, :], "%s:%d" % (wide[1], i), [wide[1]]
            else:
                dst, dkey, ww = self.sq[0:rows, s, :], "sq%d" % s, []
            if s == 0 or sq_eng == "act":
                tr.op("act", lambda e, ap=ap, dst=dst: e.activation(out=dst, in_=ap, func=AF.Square), r=[key], w=[dkey], wait_w=ww)
            else:
                tr.op("dve", lambda e, ap=ap, dst=dst: e.tensor_tensor(out=dst, in0=ap, in1=ap, op=ALU.mult), r=[key], w=[dkey], wait_w=ww)
        for i, (ap, rows, key) in enumerate(srcs):
            s = i % 2
            if wide is not None:
                src, skeys = wide[0][0:rows, i, :], ["%s:%d" % (wide[1], i), wide[1]]
            else:
                src, skeys = self.sq[0:rows, s, :], ["sq%d" % s]
            tr.op("pe", lambda e, src=src, rows=rows, i=i: e.matmul(bank_ap, lhsT=ones[0:rows, :], rhs=src,
                                                                    start=(i == 0), stop=(i == n - 1)),
                  r=skeys + ["ones_bf", "one1_bf"], w=keys_r, sig=(i == n - 1))

    def rsqrt_bank(self, bank_ap, key, mult):
        tr = self.tr
        tr.op("act", lambda e: e.activation(out=self.rstd[:], in_=bank_ap, func=AF.Ln, bias=self.epsb[:, 0:1], scale=mult),
              r=[key, "epsb"], w=["rstd"])
        tr.op("act", lambda e: e.activation(out=self.rstd[:], in_=self.rstd[:], func=AF.Exp, scale=-0.5),
              r=["rstd"], w=["rstd"])

    def prenorm(self, l, which, xkey="xt", chained=True):
        tr = self.tr
        self.sumsq([(self.xt[:, kc, :], 128, ("xt:%d" % kc) if chained else xkey) for kc in range(8)], self.bank(7), self.ones_bf, ["ps7"],
                   wide=(self.qlatT, "sqw"), sq_eng=("act" if chained else "alt"))
        self.rsqrt_bank(self.bank(7), "ps7", 1.0)
        Acol = self.coefv(l, which * 3 + 0)
        Bcol = self.coefv(l, which * 3 + 1)
        for kc in range(8):
            s = kc % 2
            tr.op("dve", lambda e, kc=kc, s=s: e.tensor_tensor(out=self.tmpf[:, s, :], in0=self.xt[:, kc, :], in1=self.rstd[:], op=ALU.mult),
                  r=[xkey, "rstd"], w=["tmpf%d" % s])
            tr.op("act", lambda e, kc=kc, s=s: e.activation(out=self.hb[:, kc, :], in_=self.tmpf[:, s, :], func=AF.Identity,
                                                            bias=Bcol[:, kc:kc + 1], scale=Acol[:, kc:kc + 1]),
                  r=["tmpf%d" % s, "coef"], w=["hb"])

    def post_sq(self, d, bk):
        self.tr.op("act", lambda e: e.activation(out=self.hb[:, d, :], in_=self.bank(bk), func=AF.Square),
                   r=["ps%d" % bk], w=["hb:%d" % d], wait_w=["hb"])

    def post_mm(self, d):
        self.tr.op("pe", lambda e: e.matmul(self.bank(7), lhsT=self.ones_bf[:, :], rhs=self.hb[:, d, :],
                                            start=(d == 0), stop=(d == 7)),
                   r=["hb:%d" % d, "hb", "ones_bf"], w=["ps7"], sig=(d == 7))

    def postnorm(self, l, which, stats_done=False):
        tr = self.tr
        if not stats_done:
            self.sumsq([(self.yt[:, kc, :], 128, "yt") for kc in range(8)], self.bank(7), self.ones_bf, ["ps7"],
                       wide=(self.hb, "hb"))
        self.rsqrt_bank(self.bank(7), "ps7", 1.0)
        Gcol = self.coefv(l, which * 3 + 2)
        for kc in range(8):
            s = kc % 2
            tr.op("dve", lambda e, kc=kc, s=s: e.scalar_tensor_tensor(
                out=self.tmpf[:, s, :], in0=self.yt[:, kc, :], scalar=Gcol[:, kc:kc + 1], in1=self.rstd[:],
                op0=ALU.mult, op1=ALU.mult), r=["yt", "rstd", "coef"], w=["tmpf%d" % s])
            tr.op("dve", lambda e, kc=kc, s=s: e.tensor_tensor(
                out=self.xt[:, kc, :], in0=self.xt[:, kc, :], in1=self.tmpf[:, s, :], op=ALU.add),
                r=["tmpf%d" % s, "xt"], w=["xt", "xt:%d" % kc])

    def ffn(self, l, which, chained=True):
        tr = self.tr
        wi = 0 if which == 1 else 2
        self.prenorm(l, wi, chained=chained)
        pb = 0
        for j in range(11):
            slot, skey = self.wnext("gu%d_%d" % (which, j))
            sv = slot[:, 0:4096].rearrange("p (kc c) -> p kc c", kc=8)
            for fc in range(2):
                bg, bu = pb % 6, (pb + 1) % 6
                pb += 2
                for (bk, off) in ((bg, fc * 128), (bu, 256 + fc * 128)):
                    for kc in range(8):
                        tr.op("pe", lambda e, bk=bk, off=off, kc=kc, sv=sv: e.matmul(
                            self.bank(bk), lhsT=sv[:, kc, off:off + 128], rhs=self.hb[:, kc, :],
                            start=(kc == 0), stop=(kc == 7)), r=[skey, "hb"], w=["ps%d" % bk], sig=(kc == 7))
                s = fc
                tr.op("act", lambda e, bg=bg, s=s: e.activation(out=self.tmpf[:, s, :], in_=self.bank(bg), func=AF.Silu),
                      r=["ps%d" % bg], w=["tmpf%d" % s])
                f = 2 * j + fc
                tr.op("dve", lambda e, bu=bu, s=s, f=f: e.tensor_tensor(
                    out=self.aT[:, f, :], in0=self.bank(bu), in1=self.tmpf[:, s, :], op=ALU.mult),
                    r=["ps%d" % bu, "tmpf%d" % s], w=(["aT", "krt", "ctile"] if f >= 20 else ["aT"]))
        for d in range(8):
            slot, skey = self.wnext("dn%d_%d" % (which, d))
            sv = slot[:, 0:22 * 128].rearrange("p (fc m) -> p fc m", m=128)
            bk = d % 6
            for fc in range(22):
                tr.op("pe", lambda e, bk=bk, fc=fc, sv=sv: e.matmul(
                    self.bank(bk), lhsT=sv[:, fc, :], rhs=self.aT[:, fc, :], start=(fc == 0), stop=(fc == 21)),
                    r=[skey, "aT"], w=["ps%d" % bk], sig=(fc == 21))
            tr.op("act", lambda e, bk=bk, d=d: e.activation(out=self.yt[:, d, :], in_=self.bank(bk), func=AF.Copy),
                  r=["ps%d" % bk], w=["yt"])
            if USE_B_FFN:
                self.post_sq(d, bk)
                if d >= 1:
                    self.post_mm(d - 1)
        if USE_B_FFN:
            self.post_mm(7)
        self.postnorm(l, wi, stats_done=USE_B_FFN)

    def kside(self, l, i):
        tr = self.tr
        c0 = i * T
        self.prenorm(l, 1)
        tr.dma("pool", self.cos_sb[:], self.cosT[:, c0:c0 + T], w=["cos"], sem="cos")
        tr.dma("pool", self.sin_sb[:], self.sinT[:, c0:c0 + T], w=["sin"], sem="sin")
        slot, skey = self.wnext("K1")
        sv = slot[:, 0:2048].rearrange("p (kc c) -> p kc c", kc=8)
        for (bk, off, m) in ((0, 0, 128), (1, 128, 64), (2, 192, 64)):
            for kc in range(8):
                tr.op("pe", lambda e, bk=bk, off=off, m=m, kc=kc: e.matmul(
                    self.bank(bk)[0:m, :], lhsT=sv[:, kc, off:off + m], rhs=self.hb[:, kc, :],
                    start=(kc == 0), stop=(kc == 7)), r=[skey, "hb"], w=["ps%d" % bk], sig=(kc == 7))
        tr.op("act", lambda e: e.activation(out=self.qaT[:, 0, :], in_=self.bank(0), func=AF.Copy), r=["ps0"], w=["qaT"])
        self.sumsq([(self.qaT[:, 0, :], 128, "qaT")], self.bank(3), self.ones_bf, ["ps3"])
        self.rsqrt_bank(self.bank(3), "ps3", 8.0)
        tr.op("dve", lambda e: e.scalar_tensor_tensor(out=self.ctile, in0=self.qaT[:, 0, :],
                                                      scalar=self.vec(l, 58, 1), in1=self.rstd[:],
                                                      op0=ALU.mult, op1=ALU.mult), r=["qaT", "rstd", "vecs"], w=["ctile"])
        tr.op("dve", lambda e: e.tensor_tensor(out=self.tmpf[0:64, 0, :], in0=self.bank(1)[0:64, :], in1=self.cos_sb[:], op=ALU.mult),
              r=["ps1", "cos"], w=["tmpf0"])
        tr.op("dve", lambda e: e.tensor_tensor(out=self.tmpf[0:64, 1, :], in0=self.bank(2)[0:64, :], in1=self.sin_sb[:], op=ALU.mult),
              r=["ps2", "sin"], w=["tmpf1"])
        tr.op("dve", lambda e: e.tensor_tensor(out=self.krt, in0=self.tmpf[0:64, 0, :], in1=self.tmpf[0:64, 1, :], op=ALU.add),
              r=["tmpf0", "tmpf1"], w=["krt"])
        self.sumsq([(self.ctile, 128, "ctile"), (self.krt, 64, "krt")], self.bank(3), self.one1_bf, ["ps3"])
        tr.op("dve", lambda e: e.tensor_reduce(out=self.small[:, 0:1], in_=self.bank(3), axis=AX.X, op=ALU.max),
              r=["ps3"], w=["small0"])
        tr.op("dve", lambda e: e.tensor_tensor(out=self.kmax2[:, l:l + 1], in0=self.kmax2[:, l:l + 1], in1=self.small[:, 0:1], op=ALU.max),
              r=["small0", "kmax2"], w=["kmax2"])
        pb = self.bank(4).bitcast(BF16)
        for j in range(4):
            tr.op("pe", lambda e, j=j: e.transpose(pb[:, j * 128:(j + 1) * 128], self.ctile[:, j * 128:(j + 1) * 128], self.ident_bf[:]),
                  r=["ctile", "ident_bf"], w=["ps4"], sig=(j == 3))
        tr.op("act", lambda e: e.activation(out=self.ctk, in_=pb[:, 0:T], func=AF.Copy), r=["ps4"], w=["ctk"])
        tr.dma("pool", self.cT_s[l][:, c0:c0 + T], self.ctile, r=["ctile"], w=["cT_s%d" % l], sem="st_c")
        tr.dma("pool", self.kr_s[l][:, c0:c0 + T], self.krt, r=["krt"], w=["kr_s%d" % l], sem="st_kr")
        tr.dma("pool", self.ctok_s[l][c0:c0 + T, :].rearrange("(j p) c -> p j c", p=128),
               self.ctk.rearrange("p (j c) -> p j c", c=128), r=["ctk"], w=["ctok_s%d" % l], sem="st_ct")
        for half in range(2):
            slot, skey = self.wnext("U%d" % half)
            sv = slot[:, 0:4096].rearrange("p (kc c) -> p kc c", kc=8)
            for cc in range(4):
                ch = half * 4 + cc
                bk = ch % 6
                for kc in range(8):
                    tr.op("pe", lambda e, bk=bk, cc=cc, kc=kc, sv=sv: e.matmul(
                        self.bank(bk), lhsT=sv[:, kc, cc * 128:(cc + 1) * 128], rhs=self.hb[:, kc, :],
                        start=(kc == 0), stop=(kc == 7)), r=[skey, "hb"], w=["ps%d" % bk], sig=(kc == 7))
                eng = "act" if ch % 2 == 0 else "dve"
                if eng == "act":
                    tr.op("act", lambda e, bk=bk, ch=ch: e.activation(out=self.yt[:, ch, :], in_=self.bank(bk), func=AF.Copy),
                          r=["ps%d" % bk], w=["yt"])
                else:
                    tr.op("dve", lambda e, bk=bk, ch=ch: e.tensor_copy(out=self.yt[:, ch, :], in_=self.bank(bk)),
                          r=["ps%d" % bk], w=["yt"])
        tr.dma("pool", self.u_s[l].rearrange("(kc p) s -> p kc s", p=128)[:, :, PAD + c0:PAD + c0 + T], self.yt[:],
               r=["yt"], w=["u_s%d" % l], sem="st_u")

    def load_kside(self, l):
        tr = self.tr
        S = self.S
        tr.dma("pool", self.wv_sb[:], self.s_kv[l][:, 1024:2048], r=["wcast%d" % l], w=["wv_sb"], sem="ld_wv")
        tr.dma("pool", self.Kc[:], self.cT_s[l][:, :], r=["cT_s%d" % l], w=["Kc"], sem="ld_kc")
        tr.dma("pool", self.Kr[0:64, :], self.kr_s[l][:, :], r=["kr_s%d" % l], w=["Kr"], sem="ld_kr")
        tr.dma("pool", self.Kt[:].rearrange("p (n c) -> p n c", c=128),
               self.ctok_s[l].rearrange("(n p) c -> p n c", p=128), r=["ctok_s%d" % l], w=["Kt"], sem="ld_kt")

    def mixer(self, l, i):
        tr = self.tr
        c0 = i * T
        self.prenorm(l, 1, chained=False)
        tr.dma("pool", self.cos_sb[:], self.cosT[:, c0:c0 + T], w=["cos"], sem="cos")
        tr.dma("pool", self.sin_sb[:], self.sinT[:, c0:c0 + T], w=["sin"], sem="sin")
        slot, skey = self.wnext("QA")
        sv = slot[:, 0:2048].rearrange("p (kc c) -> p kc c", kc=8)
        for qc in range(2):
            for kc in range(8):
                tr.op("pe", lambda e, qc=qc, kc=kc: e.matmul(self.bank(qc), lhsT=sv[:, kc, qc * 128:(qc + 1) * 128],
                                                              rhs=self.hb[:, kc, :], start=(kc == 0), stop=(kc == 7)),
                      r=[skey, "hb"], w=["ps%d" % qc], sig=(kc == 7))
            tr.op("act", lambda e, qc=qc: e.activation(out=self.qaT[:, qc, :], in_=self.bank(qc), func=AF.Copy),
                  r=["ps%d" % qc], w=["qaT"])
        self.sumsq([(self.qaT[:, 0, :], 128, "qaT"), (self.qaT[:, 1, :], 128, "qaT")], self.bank(3), self.ones_bf, ["ps3"])
        self.rsqrt_bank(self.bank(3), "ps3", 4.0)
        for qc in range(2):
            tr.op("dve", lambda e, qc=qc: e.scalar_tensor_tensor(
                out=self.qnb[:, qc, :], in0=self.qaT[:, qc, :], scalar=self.vec(l, 56 + qc, 1), in1=self.rstd[:],
                op0=ALU.mult, op1=ALU.mult), r=["qaT", "rstd", "vecs"], w=["qnb"])
        tr.op("dve", lambda e: e.memset(self.qrT_hi, 0.0), r=["qnb"], w=["qrThi"])
        wq, wqkey = self.wnext("WQB")
        wqv = wq[:, 0:4096].rearrange("p (kc c) -> p kc c", kc=2)
        wkv, wkvkey = self.wnext("WKV")
        def stage_a(h):
            s = h % 2
            bn, br, bp = s, 2 + s, 4 + s
            for kc in range(2):
                tr.op("pe", lambda e, kc=kc: e.matmul(self.bank(bn), lhsT=wqv[:, kc, h * 192:h * 192 + 128],
                                                     rhs=self.qnb[:, kc, :], start=(kc == 0), stop=(kc == 1)),
                      r=[wqkey, "qnb"], w=["ps%d" % bn], sig=(kc == 1))
            for kc in range(2):
                tr.op("pe", lambda e, kc=kc: e.matmul(self.bank(br)[0:64, :], lhsT=wqv[:, kc, h * 192 + 128:h * 192 + 192],
                                                     rhs=self.qnb[:, kc, :], start=(kc == 0), stop=(kc == 1)),
                      r=[wqkey, "qnb"], w=["ps%d" % br], sig=(kc == 1))
            for kc in range(2):
                tr.op("pe", lambda e, kc=kc: e.matmul(self.bank(bp)[0:64, :], lhsT=wqv[:, kc, 1536 + h * 64:1536 + h * 64 + 64],
                                                     rhs=self.qnb[:, kc, :], start=(kc == 0), stop=(kc == 1)),
                      r=[wqkey, "qnb"], w=["ps%d" % bp], sig=(kc == 1))
            tr.op("act", lambda e: e.activation(out=self.qnope[:, s, :], in_=self.bank(bn), func=AF.Copy),
                  r=["ps%d" % bn], w=["qnope%d" % s])
            tr.op("dve", lambda e: e.tensor_tensor(out=self.rtmp[0:64, s, 0, :], in0=self.bank(br)[0:64, :], in1=self.cos_sb[:], op=ALU.mult),
                  r=["ps%d" % br, "cos"], w=["rtmp%da" % s], wait_w=["yt"])
            tr.op("dve", lambda e: e.tensor_tensor(out=self.rtmp[0:64, s, 1, :], in0=self.bank(bp)[0:64, :], in1=self.sin_sb[:], op=ALU.mult),
                  r=["ps%d" % bp, "sin"], w=["rtmp%db" % s], wait_w=["yt"])
            tr.op("dve", lambda e: e.tensor_tensor(out=self.qrT[:, h, :], in0=self.rtmp[0:64, s, 0, :], in1=self.rtmp[0:64, s, 1, :], op=ALU.add),
                  r=["rtmp%da" % s, "rtmp%db" % s, "yt"], w=["qrT%d" % h])

        def stage_b(h):
            s = h % 2
            tr.op("pe", lambda e: e.matmul(self.bank(6), lhsT=wkv[:, h * 128:(h + 1) * 128], rhs=self.qnope[:, s, :],
                                           start=True, stop=True), r=[wkvkey, "qnope%d" % s], w=["ps6"])
            tr.op("act", lambda e: e.activation(out=self.qlatT[:, h, :], in_=self.bank(6), func=AF.Copy),
                  r=["ps6"], w=["qlatT%d" % h])

        def stage_c(h):
            s = h % 2
            self.sumsq([(self.qlatT[:, h, :], 128, "qlatT%d" % h), (self.qrT[:, h, :], 64, "qrT%d" % h)],
                       self.bank(7), self.one1_bf, ["ps7"])
            c = 5 * s
            sm = lambda j: self.small[:, c + j:c + j + 1]
            tr.op("dve", lambda e: e.tensor_reduce(out=sm(0), in_=self.bank(7), axis=AX.X, op=ALU.max),
                  r=["ps7"], w=["small%d" % c])
            tr.op("dve", lambda e: e.tensor_tensor(out=sm(1), in0=sm(0), in1=self.kmax2[:, l:l + 1], op=ALU.mult),
                  r=["small%d" % c, "kmax2"], w=["small%d" % c])
            tr.op("act", lambda e: e.activation(out=sm(2), in_=sm(1), func=AF.Ln), r=["small%d" % c], w=["small%d" % c])
            tr.op("act", lambda e: e.activation(out=sm(3), in_=sm(2), func=AF.Exp, scale=0.5), r=["small%d" % c], w=["small%d" % c])
            tr.op("dve", lambda e: e.tensor_scalar(out=self.negM[:, h:h + 1], in0=sm(3), scalar1=-SCALE, scalar2=None,
                                                   op0=ALU.mult), r=["small%d" % c], w=["negM%d" % h])

        for h in range(NH + 2):
            if h < NH:
                stage_a(h)
            if 1 <= h <= NH:
                stage_b(h - 1)
            if h >= 2:
                stage_c(h - 2)
        W = T + 2 * PAD
        u_dram = self.u_s[l].rearrange("(kc p) s -> p kc s", p=128)

        def load_u(g):
            bsel = g % 2
            tr.dma("pool", self.ut[:, bsel], u_dram[:, 2 * g:2 * g + 2, c0:c0 + W],
                   r=["u_s%d" % l, "upad"], w=["ut%d" % bsel], sem="ld_u%d" % bsel)
            tr.dma("pool", self.invc_sb[:, bsel, :], self.invc[:, g, c0:c0 + T], w=["invc%d" % bsel], sem="ld_ic%d" % bsel)

        load_u(0)
        load_u(1)

        def win(g):
            bsel = g % 2
            utk, ick = "ut%d" % bsel, "invc%d" % bsel
            for cc in range(2):
                u = self.ut[:, bsel, cc, :]
                a, b = self.pw[:, 0, :], self.pw[:, 1, :]
                ka, kb = "pw0", "pw1"
                eng = "dve"
                tr.op(eng, lambda e, u=u, a=a: e.tensor_tensor(out=a[:, 1:W], in0=u[:, 0:W - 1], in1=u[:, 1:W], op=ALU.add),
                      r=[utk], w=[ka])
                cur, ck, oth, ok = a, ka, b, kb
                lo = 1
                hw = 1
                for lev in range(g):
                    nlo, nhi = lo + hw, W - hw
                    tr.op(eng, lambda e, cur=cur, oth=oth, nlo=nlo, nhi=nhi, hw=hw: e.tensor_tensor(
                        out=oth[:, nlo:nhi], in0=cur[:, nlo - hw:nhi - hw], in1=cur[:, nlo + hw:nhi + hw], op=ALU.add),
                        r=[ck], w=[ok])
                    cur, ck, oth, ok = oth, ok, cur, ck
                    lo = nlo
                    hw *= 2
                tr.op(eng, lambda e, cur=cur, oth=oth: e.tensor_tensor(out=oth[:, PAD:PAD + T], in0=cur[:, PAD:PAD + T],
                                                                      in1=self.invc_sb[:, bsel, :], op=ALU.mult),
                      r=[ck, ick], w=[ok])
                tr.op(eng, lambda e, oth=oth, u=u, cc=cc: e.tensor_tensor(out=self.pooled[:, cc, :], in0=oth[:, PAD:PAD + T],
                                                                           in1=u[:, PAD:PAD + T], op=ALU.subtract),
                      r=[ok, utk], w=["pooled%d" % cc])
            if g + 2 < 4:
                load_u(g + 2)

        def mm(g):
            if g % 2 == 0:
                wp, self._wpkey = self.wnext("WPOOL%d" % (g // 2))
                self._wpv = wp[:, 0:2048].rearrange("p (g kc m) -> p g kc m", g=4, kc=2)
                gp, self._gpkey = self.wnext("GP%d" % (g // 2))
                self._gpv = gp[:, 0:4096].rearrange("p (kc c) -> p kc c", kc=8)
            wpv, wpkey, gpv, gpkey = self._wpv, self._wpkey, self._gpv, self._gpkey
            self.dump("pooled_g%d" % g, self.pooled[:], ["pooled0", "pooled1"], BF16)
            later = []
            for j in range(2):
                ch = 2 * g + j
                for kc in range(2):
                    tr.op("pe", lambda e, j=j, kc=kc: e.matmul(self.bank(j), lhsT=wpv[:, g, kc, j * 128:(j + 1) * 128],
                                                               rhs=self.pooled[:, kc, :], start=(kc == 0), stop=(kc == 1)),
                          r=[wpkey, "pooled0", "pooled1"], w=["ps%d" % j], sig=(kc == 1))
                gc = (g % 2) * 2 + j
                for kc in range(8):
                    tr.op("pe", lambda e, gc=gc, kc=kc, j=j: e.matmul(self.bank(2 + j), lhsT=gpv[:, kc, gc * 128:(gc + 1) * 128],
                                                                     rhs=self.hb[:, kc, :], start=(kc == 0), stop=(kc == 7)),
                          r=[gpkey, "hb"], w=["ps%d" % (2 + j)], sig=(kc == 7))
                self.sigmoid(self.bank(2 + j), "ps%d" % (2 + j), j)
                later.append(lambda j=j, ch=ch: tr.op("dve", lambda e: e.scalar_tensor_tensor(
                    out=self.yt[:, ch, :], in0=self.bank(j), scalar=self.vec(l, 48 + ch, 1), in1=self.tmpf[:, j, :],
                    op0=ALU.mult, op1=ALU.mult), r=["ps%d" % j, "tmpf%d" % j, "vecs"], w=["yt"]))
            return later

        win(0)
        for g in range(4):
            later = mm(g)
            if g + 1 < 4:
                win(g + 1)
            for f in later:
                f()
        self.dump("pool", self.yt[:], ["yt"])
        self.dump("qlatT", self.qlatT, ["qlatT%d" % h for h in range(NH)], BF16)
        self.dump("qrT", self.qrT, ["qrT%d" % h for h in range(NH)], BF16)
        self.dump("negM", self.negM[:], ["negM%d" % h for h in range(NH)])
        self.attention(l, i)
        self.dump("merged", self.merged[:], ["merged"], BF16)
        for half in range(2):
            wo, wokey = self.wnext("WO%d" % half)
            wov = wo[:, 0:4096].rearrange("p (kc c) -> p kc c", kc=8)
            for cc in range(4):
                d = half * 4 + cc
                bk = d % 6
                for kc in range(8):
                    tr.op("pe", lambda e, bk=bk, cc=cc, kc=kc, wov=wov: e.matmul(
                        self.bank(bk), lhsT=wov[:, kc, cc * 128:(cc + 1) * 128], rhs=self.merged[:, kc, :],
                        start=(kc == 0), stop=(kc == 7)), r=[wokey, "merged"], w=["ps%d" % bk], sig=(kc == 7))
                tr.op("act", lambda e, bk=bk, d=d: e.activation(out=self.yt[:, d, :], in_=self.bank(bk), func=AF.Copy),
                      r=["ps%d" % bk], w=["yt"])
                if USE_B:
                    self.post_sq(d, bk)
                    if d >= 1:
                        self.post_mm(d - 1)
        if USE_B:
            self.post_mm(7)
        self.postnorm(l, 1, stats_done=USE_B)
        self.dump("x2", self.xt[:], ["xt"])

    def sigmoid(self, src, skey, s):
        tr = self.tr
        k = "tmpf%d" % s
        tr.op("act", lambda e: e.activation(out=self.tmpf[:, s, :], in_=src, func=AF.Exp, scale=-1.0), r=[skey], w=[k])
        tr.op("act", lambda e: e.activation(out=self.tmpf[:, s, :], in_=self.tmpf[:, s, :], func=AF.Ln, bias=self.oneb[:, 0:1], scale=1.0),
              r=[k, "oneb"], w=[k])
        tr.op("act", lambda e: e.activation(out=self.tmpf[:, s, :], in_=self.tmpf[:, s, :], func=AF.Exp, scale=-1.0), r=[k], w=[k])

    def attention(self, l, i):
        tr = self.tr
        NP = self.NKT // 2
        NS = NH * NP
        AHEAD = 2

        def ptile(sidx):
            j = sidx % 3
            return self.PTs[:, j * 2 * T:(j + 1) * 2 * T], "PT%d" % j

        def qk(sidx):
            h, pp = divmod(sidx, NP)
            b = (sidx % 3) * 2
            for t2 in range(2):
                kt = pp * 2 + t2
                extra = ["PT%d" % ((sidx - AHEAD) % 3)] if (t2 == 0 and sidx >= AHEAD) else []
                tr.op("pe", lambda e, kt=kt, t2=t2: e.matmul(self.bank(b + t2), lhsT=self.Kc[:, kt * 128:(kt + 1) * 128],
                                                            rhs=self.qlatT[:, h, :], start=True, stop=False),
                      r=["Kc", "qlatT%d" % h] + extra, w=["ps%d" % (b + t2)], sig=False)
                tr.op("pe", lambda e, kt=kt, t2=t2: e.matmul(self.bank(b + t2), lhsT=self.Kr[:, kt * 128:(kt + 1) * 128],
                                                            rhs=self.qrT_full[:, h, :], start=False, stop=True),
                      r=["Kr", "Krz", "qrT%d" % h, "qrThi"], w=["ps%d" % (b + t2)], sig=(t2 == 1))
            pt, ptk = ptile(sidx)
            tr.op("act", lambda e: e.activation(out=pt, in_=self.bank(b, 2), func=AF.Exp, bias=self.negM[:, h:h + 1], scale=SCALE),
                  r=["ps%d" % b, "ps%d" % (b + 1), "negM%d" % h], w=[ptk])

        def pv(sidx):
            h, pp = divmod(sidx, NP)
            pt, ptk = ptile(sidx)
            for t2 in range(2):
                kt = pp * 2 + t2
                tr.op("pe", lambda e, kt=kt, t2=t2: e.matmul(self.bank(6), lhsT=self.Kt[:, kt * 128:(kt + 1) * 128],
                                                            rhs=pt[:, t2 * T:(t2 + 1) * T],
                                                            start=(kt == 0), stop=(kt == self.NKT - 1)),
                      r=["Kt", ptk], w=["ps6"], sig=(t2 == 1))
            acc = self.accd[:, (h % 2) * T:(h % 2 + 1) * T]
            acck = "accd%d" % (h % 2)
            if pp == 0:
                tr.op("dve", lambda e: e.tensor_tensor(out=acc, in0=pt[:, 0:T], in1=pt[:, T:2 * T], op=ALU.add), r=[ptk], w=[acck])
            else:
                tr.op("dve", lambda e: e.tensor_tensor(out=self.ptsum[:], in0=pt[:, 0:T], in1=pt[:, T:2 * T], op=ALU.add),
                      r=[ptk], w=["ptsum"])
                tr.op("dve", lambda e: e.tensor_tensor(out=acc, in0=acc, in1=self.ptsum[:], op=ALU.add), r=["ptsum", acck], w=[acck])

        def ep0(h):
            tr.op("dve", lambda e: e.tensor_copy(out=self.ovs[:], in_=self.bank(6)), r=["ps6"], w=["ovs"])

        def ep1(h):
            acc = self.accd[:, (h % 2) * T:(h % 2 + 1) * T]
            tr.op("pe", lambda e: e.matmul(self.bank(7), lhsT=self.ones_f[:], rhs=acc, start=True, stop=True),
                  r=["accd%d" % (h % 2), "ones_f"], w=["ps7"])
            tr.op("act", lambda e: e.activation(out=self.rstd[:], in_=self.bank(7), func=AF.Ln), r=["ps7"], w=["rstd"])
            tr.op("act", lambda e: e.activation(out=self.rstd[:], in_=self.rstd[:], func=AF.Exp, scale=-1.0), r=["rstd"], w=["rstd"])
            tr.op("dve", lambda e: e.tensor_tensor(out=self.olat[:], in0=self.ovs[:], in1=self.rstd[:], op=ALU.mult),
                  r=["ovs", "rstd"], w=["olat"])

        def gates(h):
            if h % 4 == 0:
                ga, self._gakey = self.wnext("GA%d" % (h // 4))
                self._gav = ga[:, 0:4096].rearrange("p (kc c) -> p kc c", kc=8)
            gav, gakey = self._gav, self._gakey
            gc = h % 4
            for kc in range(8):
                tr.op("pe", lambda e, kc=kc: e.matmul(self.bank(7), lhsT=gav[:, kc, gc * 128:(gc + 1) * 128], rhs=self.hb[:, kc, :],
                                                     start=(kc == 0), stop=(kc == 7)), r=[gakey, "hb"], w=["ps7"], sig=(kc == 7))
            self.sigmoid(self.bank(7), "ps7", 0)

        def ep2(h):
            tr.op("pe", lambda e: e.matmul(self.bank(7), lhsT=self.wv_sb[:, h * 128:(h + 1) * 128], rhs=self.olat[:],
                                           start=True, stop=True), r=["wv_sb", "olat"], w=["ps7"])
            tr.op("dve", lambda e: e.tensor_tensor(out=self.tmpf[:, 1, :], in0=self.bank(7), in1=self.tmpf[:, 0, :], op=ALU.mult),
                  r=["ps7", "tmpf0"], w=["tmpf1"])
            tr.op("dve", lambda e: e.tensor_tensor(out=self.merged[:, h, :], in0=self.tmpf[:, 1, :], in1=self.yt[:, h, :], op=ALU.add),
                  r=["tmpf1", "yt"], w=["merged"])

        fns = {0: ep0, 1: ep1, 2: ep2, 3: gates}
        todo = []
        off2 = max(1, min(5, NP // 2 - 1))
        for sidx in range(min(AHEAD, NS)):
            qk(sidx)
        for sidx in range(NS):
            h, pp = divmod(sidx, NP)
            if pp == 0:
                todo.append((sidx + NP // 2, 3, h))
            if sidx + AHEAD < NS:
                qk(sidx + AHEAD)
            pv(sidx)
            if pp == NP - 1:
                ep0(h)
                todo.append((sidx + 1, 1, h))
                todo.append((sidx + off2, 2, h))
            while todo and todo[0][0] <= sidx:
                _, kind, hh = todo.pop(0)
                fns[kind](hh)
        for _, kind, hh in todo:
            fns[kind](hh)

    def build(self):
        tr, L, NT = self.tr, self.L, self.NT
        for i in range(NT):
            self.plan(self.plan_ffn(0, 1) + self.plan_kside(0))
        for l in range(L):
            for i in range(NT):
                self.plan(self.plan_mixer(l) + self.plan_ffn(l, 2))
                if l < L - 1:
                    self.plan(self.plan_ffn(l + 1, 1) + self.plan_kside(l + 1))
        self.prologue()
        xTv = self.xT.rearrange("(kc p) s -> p kc s", p=128)
        x1v = self.x1_s.rearrange("(kc p) s -> p kc s", p=128)
        yTv = self.yT.rearrange("(kc p) s -> p kc s", p=128)
        for i in range(NT):
            tr.dma("pool", self.xt[:], xTv[:, :, i * T:(i + 1) * T], w=["xt"], sem="ld_x")
            self.ffn(0, 1, chained=False)
            tr.dma("pool", x1v[:, :, i * T:(i + 1) * T], self.xt[:], r=["xt"], w=["x1_s"], sem="st_x")
            self.kside(0, i)
        for l in range(L):
            self.load_kside(l)
            for i in range(NT):
                tr.dma("pool", self.xt[:], x1v[:, :, i * T:(i + 1) * T], r=["x1_s"], w=["xt"], sem="ld_x")
                self.mixer(l, i)
                self.ffn(l, 2)
                if l < L - 1:
                    self.ffn(l + 1, 1)
                    tr.dma("pool", x1v[:, :, i * T:(i + 1) * T], self.xt[:], r=["xt"], w=["x1_s"], sem="st_x")
                    self.kside(l + 1, i)
                else:
                    tr.dma("pool", yTv[:, :, i * T:(i + 1) * T], self.xt[:], r=["xt"], w=["yT"], sem="st_y")
        tr.wait_all("pool", ["yT", "x1_s"] + ["dbg_" + n for n in self._dumped])
        assert self.wpos == len(self.wplan)
        return self.nc


def _rope_tables(S):
    inv = (1.0 / (np.float32(10000.0) ** (np.arange(0, 64, 2, dtype=np.float32) / np.float32(64)))).astype(np.float32)
    ang = np.arange(S, dtype=np.float32)[:, None] * inv[None, :]
    ang = np.concatenate([ang, ang], axis=-1).astype(np.float32)
    cos = np.cos(ang).astype(np.float32)
    sin = np.sin(ang).astype(np.float32)
    sgn = np.concatenate([-np.ones(32, np.float32), np.ones(32, np.float32)])
    return np.ascontiguousarray(cos.T), np.ascontiguousarray((sin * sgn[None, :]).T)


def _invcnt(S):
    t = np.arange(S)[:, None]
    half = np.array([1, 2, 4, 8])[None, :]
    lo = np.clip(t - half, 0, S)
    hi = np.clip(t + half, 0, S)
    ic = (1.0 / (hi - lo).astype(np.float32)).astype(np.float32)
    return np.ascontiguousarray(np.broadcast_to(ic.T[None, :, :], (128, 4, S)))


def host_inputs(S, L, x_seq, c_seq, P):
    fm = lambda v: np.ascontiguousarray(np.asarray(v, np.float32).reshape(-1, 128).T)
    cols = []
    for l in range(L):
        cols += [fm(P["n1_pre"][l]), fm(P["n1_post"][l]), fm(P["nm_pre"][l]), fm(P["nm_post"][l]),
                 fm(P["n2_pre"][l]), fm(P["n2_post"][l]), fm(P["pool_scale"][l]), fm(P["q_a_norm"][l]),
                 fm(P["kv_a_norm"][l]), fm(P["b_ada"][l])]
    vecs = np.ascontiguousarray(np.concatenate(cols, axis=1))
    assert vecs.shape == (128, L * NV)
    wk = np.asarray(P["w_kvb"][:L], np.float32).reshape(L, 128, 8, 256)[:, :, :, 0:128]
    wkT = np.ascontiguousarray(wk.transpose(0, 3, 2, 1)).reshape(L, 128, 1024)
    cosT, sinT = _rope_tables(S)
    f = lambda k: np.ascontiguousarray(np.asarray(P[k][:L], np.float32))
    return {
        "xT": np.ascontiguousarray(np.asarray(x_seq, np.float32).T),
        "cT": fm(c_seq),
        "vecs": vecs,
        "w_ada": f("w_ada"), "w1_gu": f("w1_gu"), "w1_down": f("w1_down"), "w2_gu": f("w2_gu"), "w2_down": f("w2_down"),
        "w_in": f("w_in"), "w_qb": f("w_qb"), "w_kvb": f("w_kvb"), "wkT": wkT, "w_pool": f("w_pool"), "w_out": f("w_out"),
        "cosT": cosT, "sinT": sinT, "invc": _invcnt(S), "ident": np.eye(128, dtype=np.float32),
    }


_WKEYS = ["w_ada", "b_ada", "n1_pre", "w1_gu", "w1_down", "n1_post", "nm_pre", "w_in", "q_a_norm", "w_qb", "kv_a_norm",
          "w_kvb", "w_pool", "pool_scale", "w_out", "nm_post", "n2_pre", "w2_gu", "w2_down", "n2_post"]


def kernel(x_prompt, x_sample, c_prompt, c_sample, **P):
    x_prompt = np.asarray(x_prompt); x_sample = np.asarray(x_sample)
    c_prompt = np.asarray(c_prompt); c_sample = np.asarray(c_sample)
    S = x_prompt.shape[1]
    L = np.asarray(P["w_ada"]).shape[0]
    P = {k: np.asarray(v) for k, v in P.items()}
    seqs = [(x_prompt[b], c_prompt[b]) for b in range(x_prompt.shape[0])] + \
           [(x_sample[b], c_sample[b]) for b in range(x_sample.shape[0])]
    n_real = len(seqs)
    while len(seqs) < 8:
        seqs.append(seqs[len(seqs) - n_real])
    base = host_inputs(S, L, seqs[0][0], seqs[0][1], P)
    in_maps = []
    fm = lambda v: np.ascontiguousarray(np.asarray(v, np.float32).reshape(-1, 128).T)
    for (xs, cs) in seqs[:8]:
        m = dict(base)
        m["xT"] = np.ascontiguousarray(np.asarray(xs, np.float32).T)
        m["cT"] = fm(cs)
        in_maps.append(m)
    nc = Builder(S, L).build()
    res = run_bass_kernel_spmd(nc, in_maps, core_ids=list(range(8)))
    outs = [np.ascontiguousarray(res.results[c]["yT"].T).astype(np.float32) for c in range(n_real)]
    nb = x_prompt.shape[0]
    return (np.stack(outs[:nb], 0), np.stack(outs[nb:], 0))
```
